# Optimizing a Trainium2 kernel written in Bass

```python
import math
import jax, jax.numpy as jnp
from jax import lax
import numpy as np

D_MODEL = 1024
BATCH = 16
SEQ = 4096
DEPTH = 2

N_A_LAYERS = DEPTH // 2
N_B_LAYERS = DEPTH - N_A_LAYERS
HEAD_DIM = 64
MEM_LEN = 256
MEM_HEADS = 4
MEM_WIDTH = MEM_HEADS * HEAD_DIM
MIX_WIDTH = D_MODEL
TOK_WIDTH = MIX_WIDTH - MEM_WIDTH
POOL_WINDOWS = (2, 4, 8, 16)
POOL_GROUP = TOK_WIDTH // len(POOL_WINDOWS)
FOX_HEADS = TOK_WIDTH // HEAD_DIM
Q_BLOCK = 128
D_FF = ((8 * D_MODEL // 3 + 63) // 64) * 64
CONV_WIDTH = 3
DN_ALPHA = (2.0 * DEPTH) ** 0.25
DN_BETA = (8.0 * DEPTH) ** -0.25
LN_EPS = 1e-5

kernel_name = "yoco_pool_fox_memory_convffn"


def layer_norm(x, g, b):
    xf = x.astype(jnp.float32)
    mu = jnp.mean(xf, axis=-1, keepdims=True)
    var = jnp.mean(jnp.square(xf - mu), axis=-1, keepdims=True)
    y = (xf - mu) * lax.rsqrt(var + LN_EPS) * g.astype(jnp.float32) + b.astype(jnp.float32)
    return y.astype(x.dtype)


def causal_mean_minus_self(u, w):
    S = u.shape[1]
    uf = u.astype(jnp.float32)
    csum = jnp.cumsum(uf, axis=1)
    lag = jnp.pad(csum, ((0, 0), (w, 0), (0, 0)))[:, :S]
    count = jnp.minimum(jnp.arange(1, S + 1), w).astype(jnp.float32)[None, :, None]
    return ((csum - lag) / count - uf).astype(u.dtype)


def multiscale_pool(u, pool_w, pool_scale):
    B, S, _ = u.shape
    ug = u.reshape(B, S, len(POOL_WINDOWS), POOL_GROUP)
    pooled = jnp.stack([causal_mean_minus_self(ug[:, :, i], w)
                        for i, w in enumerate(POOL_WINDOWS)], axis=2)
    mixed = jnp.einsum('bsgc,gcd->bsgd', pooled, pool_w)
    return mixed.reshape(B, S, TOK_WIDTH) * pool_scale


def memory_attention(q_mem, mem_k, mem_v):
    B, S = q_mem.shape[:2]
    logits = jnp.einsum('bshd,bmhd->bhsm', q_mem, mem_k).astype(jnp.float32) * (HEAD_DIM ** -0.5)
    p = jax.nn.softmax(logits, axis=-1)
    out = jnp.einsum('bhsm,bmhd->bshd', p.astype(mem_v.dtype), mem_v)
    return out.reshape(B, S, MEM_WIDTH)


def forgetting_attention(q, k, v, F):
    B, S, H, Dh = q.shape
    nb = S // Q_BLOCK
    scale = Dh ** -0.5
    qb = q.reshape(B, nb, Q_BLOCK, H, Dh).transpose(1, 0, 2, 3, 4)
    Fqb = F.reshape(B, nb, Q_BLOCK, H).transpose(1, 0, 3, 2)
    Fk = F.transpose(0, 2, 1)[:, :, None, :]
    k_pos = jnp.arange(S)

    def block(args):
        qi, Fqi, start = args
        logits = jnp.einsum('bqhd,bkhd->bhqk', qi, k).astype(jnp.float32) * scale
        logits = logits + Fqi[..., None] - Fk
        q_pos = start + jnp.arange(Q_BLOCK)
        mask = q_pos[:, None] >= k_pos[None, :]
        logits = jnp.where(mask, logits, -jnp.inf)
        p = jax.nn.softmax(logits, axis=-1)
        return jnp.einsum('bhqk,bkhd->bqhd', p.astype(v.dtype), v)

    starts = jnp.arange(nb) * Q_BLOCK
    out = lax.map(block, (qb, Fqb, starts))
    return out.transpose(1, 0, 2, 3, 4).reshape(B, S, H * Dh)


def conv_ffn(x, w_up, conv_w, conv_b, w_down):
    h = x @ w_up
    C = h.shape[-1]
    h = lax.conv_general_dilated(h, conv_w[:, None, :].astype(h.dtype), window_strides=(1,),
                                 padding=[(CONV_WIDTH - 1, 0)],
                                 dimension_numbers=('NWC', 'WIO', 'NWC'),
                                 feature_group_count=C) + conv_b
    u, g = jnp.split(h, 2, axis=-1)
    return (jax.nn.silu(g) * u) @ w_down


def setup_inputs(seed: int = 0) -> dict:
    key = jax.random.key(seed)
    ks = jax.random.split(key, 24)
    f32 = jnp.float32

    def nrm(k, shape, fan_in, gain=1.0):
        return jax.random.normal(k, shape, f32) * (gain * fan_in ** -0.5)

    x = jax.random.normal(ks[0], (BATCH, SEQ, D_MODEL), f32)
    mem = jax.random.normal(ks[1], (BATCH, MEM_LEN, D_MODEL), f32)
    a_w_in = nrm(ks[2], (N_A_LAYERS, D_MODEL, MIX_WIDTH), D_MODEL)
    a_pool_w = nrm(ks[3], (N_A_LAYERS, len(POOL_WINDOWS), POOL_GROUP, POOL_GROUP), POOL_GROUP)
    a_pool_scale = 1.0 + 0.1 * jax.random.normal(ks[4], (N_A_LAYERS, TOK_WIDTH), f32)
    a_w_out = nrm(ks[5], (N_A_LAYERS, MIX_WIDTH, D_MODEL), MIX_WIDTH, DN_BETA)
    b_w_q = nrm(ks[6], (N_B_LAYERS, D_MODEL, MIX_WIDTH), D_MODEL)
    b_w_out = nrm(ks[7], (N_B_LAYERS, MIX_WIDTH, D_MODEL), MIX_WIDTH, DN_BETA)
    kv_w = jnp.concatenate([nrm(ks[8], (D_MODEL, 2 * TOK_WIDTH), D_MODEL),
                            nrm(ks[9], (D_MODEL, FOX_HEADS), D_MODEL, 0.5)], axis=-1)
    f_b = jax.random.uniform(ks[10], (FOX_HEADS,), f32, 1.0, 4.0)
    mem_w_kv = nrm(ks[11], (DEPTH, D_MODEL, 2 * MEM_WIDTH), D_MODEL)
    ln1_g = 1.0 + 0.05 * jax.random.normal(ks[12], (DEPTH, D_MODEL), f32)
    ln1_b = 0.02 * jax.random.normal(ks[13], (DEPTH, D_MODEL), f32)
    ln2_g = 1.0 + 0.05 * jax.random.normal(ks[14], (DEPTH, D_MODEL), f32)
    ln2_b = 0.02 * jax.random.normal(ks[15], (DEPTH, D_MODEL), f32)
    ffn_w_up = nrm(ks[16], (DEPTH, D_MODEL, 2 * D_FF), D_MODEL)
    ffn_conv_w = nrm(ks[17], (DEPTH, CONV_WIDTH, 2 * D_FF), CONV_WIDTH)
    ffn_conv_b = 0.02 * jax.random.normal(ks[18], (DEPTH, 2 * D_FF), f32)
    ffn_w_down = nrm(ks[19], (DEPTH, D_FF, D_MODEL), D_FF, DN_BETA)
    return {"x": x, "mem": mem, "a_w_in": a_w_in, "a_pool_w": a_pool_w,
            "a_pool_scale": a_pool_scale, "a_w_out": a_w_out, "b_w_q": b_w_q,
            "b_w_out": b_w_out, "kv_w": kv_w, "f_b": f_b, "mem_w_kv": mem_w_kv,
            "ln1_g": ln1_g, "ln1_b": ln1_b, "ln2_g": ln2_g, "ln2_b": ln2_b,
            "ffn_w_up": ffn_w_up, "ffn_conv_w": ffn_conv_w, "ffn_conv_b": ffn_conv_b,
            "ffn_w_down": ffn_w_down}


def reference(x, mem, a_w_in, a_pool_w, a_pool_scale, a_w_out, b_w_q, b_w_out, kv_w, f_b,
              mem_w_kv, ln1_g, ln1_b, ln2_g, ln2_b, ffn_w_up, ffn_conv_w, ffn_conv_b,
              ffn_w_down):
    B, S, _ = x.shape
    M = mem.shape[1]
    k_sh = v_sh = F_sh = None
    for l in range(DEPTH):
        mem_kv = mem @ mem_w_kv[l]
        mem_k = mem_kv[..., :MEM_WIDTH].reshape(B, M, MEM_HEADS, HEAD_DIM)
        mem_v = mem_kv[..., MEM_WIDTH:].reshape(B, M, MEM_HEADS, HEAD_DIM)

        if l < N_A_LAYERS:
            proj = x @ a_w_in[l]
            tok = multiscale_pool(proj[..., :TOK_WIDTH], a_pool_w[l], a_pool_scale[l])
            w_out = a_w_out[l]
        else:
            if l == N_A_LAYERS:
                kvf = x @ kv_w
                k_sh = kvf[..., :TOK_WIDTH].reshape(B, S, FOX_HEADS, HEAD_DIM)
                v_sh = kvf[..., TOK_WIDTH:2 * TOK_WIDTH].reshape(B, S, FOX_HEADS, HEAD_DIM)
                log_f = jax.nn.log_sigmoid(kvf[..., 2 * TOK_WIDTH:].astype(jnp.float32)
                                           + f_b.astype(jnp.float32))
                F_sh = jnp.cumsum(log_f, axis=1)
            j = l - N_A_LAYERS
            proj = x @ b_w_q[j]
            q = proj[..., :TOK_WIDTH].reshape(B, S, FOX_HEADS, HEAD_DIM)
            tok = forgetting_attention(q, k_sh, v_sh, F_sh)
            w_out = b_w_out[j]

        q_mem = proj[..., TOK_WIDTH:].reshape(B, S, MEM_HEADS, HEAD_DIM)
        mem_out = memory_attention(q_mem, mem_k, mem_v)
        mix = jnp.concatenate([tok, mem_out], axis=-1) @ w_out
        x = layer_norm(DN_ALPHA * x + mix, ln1_g[l], ln1_b[l])

        ffn = conv_ffn(x, ffn_w_up[l], ffn_conv_w[l], ffn_conv_b[l], ffn_w_down[l])
        x = layer_norm(DN_ALPHA * x + ffn, ln2_g[l], ln2_b[l])
    return x
```

```python
import numpy as np
import ml_dtypes
from contextlib import ExitStack
import concourse.bass as bass
import concourse.mybir as mybir
from concourse.bass_utils import run_bass_kernel_spmd

F32 = mybir.dt.float32
BF16 = mybir.dt.bfloat16
AF = mybir.ActivationFunctionType
ALU = mybir.AluOpType

D = 1024
KC = 8
T = 512
DFF = 2752
NPC = 22
NH = 12
MEM = 256
ALPHA = float((2.0 * 2) ** 0.25)
EPS = 1e-5
USZ = 4096
RING = 4
NEG = -30000.0

UNIT_NAMES = (["AIN0", "AIN1", "AOUT0", "AOUT1"] + ["UP0_%d" % i for i in range(11)]
              + ["DN0_%d_%d" % (hf, gi) for hf in range(2) for gi in range(3)]
              + ["KV0", "KV1", "KV2", "BQ0", "BQ1", "BQM", "BOUT0", "BOUT1"]
              + ["UP1_%d" % i for i in range(11)]
              + ["DN1_%d_%d" % (hf, gi) for hf in range(2) for gi in range(3)]
              + ["MKV0", "MKV1"])
UIDX = {n: i for i, n in enumerate(UNIT_NAMES)}
NU = len(UNIT_NAMES)


def unit_ncols(name):
    if name.startswith("BQ") and name != "BQM":
        return 390
    if name == "BQM":
        return 256
    return 512


def _cols_layout():
    m = {}
    n = 0
    for l in range(2):
        for nm in ("ln1g", "ln1b", "ln2g", "ln2b"):
            m[(nm, l)] = n
            n += 8
    m["pscale"] = n
    n += 6
    for l in range(2):
        for k in range(3):
            m[("cw", l, k)] = n
            n += 44
        m[("cb", l)] = n
        n += 44
    return m, n


COLMAP, NCOLS = _cols_layout()
C_U, C_ID, C_MASK, C_INVC, C_E = 0, 128, 256, 384, 448
C_B = 448 + 780
CW = C_B + 12 * 128


class Sched:
    def __init__(self):
        self.ops = []
        self.res = {}

    def add(self, eng, fn, reads=(), writes=(), dsem=None):
        idx = len(self.ops)
        deps = set()
        for r in reads:
            st = self.res.get(r)
            if st is not None and st[0] is not None:
                deps.add(st[0])
            if st is not None and isinstance(r, tuple) and r[0] == "ps":
                for (e2, _d), ix in st[1].items():
                    if e2 != eng:
                        deps.add(ix)
        for w in writes:
            st = self.res.get(w)
            if st is not None:
                if st[0] is not None:
                    deps.add(st[0])
                deps.update(st[1].values())
        deps.discard(idx)
        self.ops.append(dict(eng=eng, fn=fn, deps=sorted(deps), dsem=dsem, needed=False, val=None))
        for r in reads:
            st = self.res.setdefault(r, [None, {}])
            st[1][(eng, dsem)] = idx
        for w in writes:
            self.res[w] = [idx, {}]
        return idx

    def emit(self, nc, stack):
        ops = self.ops
        for op in ops:
            for d in op["deps"]:
                ops[d]["needed"] = True
        esem = {}
        for e in ("pe", "act", "dve", "pool"):
            esem[e] = stack.enter_context(nc.semaphore("s_" + e))
        dsem = {}
        cnt = {e: 0 for e in esem}
        dcnt = {}
        for op in ops:
            if op["dsem"] is not None:
                if op["dsem"] not in dsem:
                    dsem[op["dsem"]] = stack.enter_context(nc.semaphore("d_" + str(op["dsem"])))
                    dcnt[op["dsem"]] = 0
                dcnt[op["dsem"]] += 1
                op["val"] = 16 * dcnt[op["dsem"]]
            elif op["needed"]:
                cnt[op["eng"]] += 1
                op["val"] = cnt[op["eng"]]
        self.stats = dict(cnt=cnt, nops=len(ops))
        by_eng = {e: [] for e in ("pe", "act", "dve", "pool", "sp")}
        for i, op in enumerate(ops):
            by_eng[op["eng"]].append(i)

        def run(ename, e):
            seen = {}
            for i in by_eng[ename]:
                op = ops[i]
                need = {}
                for d in op["deps"]:
                    dop = ops[d]
                    if dop["dsem"] is not None:
                        key = ("d", dop["dsem"])
                    else:
                        key = ("e", dop["eng"])
                        if dop["eng"] == "pe" and ename == "pe":
                            continue
                    if dop["val"] > need.get(key, 0):
                        need[key] = dop["val"]
                for key, v in need.items():
                    if seen.get(key, 0) >= v:
                        continue
                    seen[key] = v
                    s = dsem[key[1]] if key[0] == "d" else esem[key[1]]
                    e.wait_ge(s, v)
                ins = op["fn"](e)
                if op["dsem"] is not None:
                    ins.then_inc(dsem[op["dsem"]], 16)
                elif op["needed"]:
                    ins.then_inc(esem[ename], 1)
            if ename == "sp":
                for k, s in dsem.items():
                    e.wait_ge(s, 16 * dcnt[k])

        with nc.Block() as block:
            @block.sync
            def _(e):
                run("sp", e)

            @block.tensor
            def _(e):
                run("pe", e)

            @block.scalar
            def _(e):
                run("act", e)

            @block.vector
            def _(e):
                run("dve", e)

            @block.gpsimd
            def _(e):
                run("pool", e)


class Rot:
    def __init__(self, name, n):
        self.name, self.n, self.i = name, n, -1

    def next(self):
        self.i = (self.i + 1) % self.n
        return self.i, (self.name, self.i)


def build_program(NSEQ, S, debug=False):
    NT = S // T
    NKT = S // 128
    nc = bass.Bass("TRN2", target_bir_lowering=False)
    stack = ExitStack()
    sch = Sched()

    def dram(name, shape, dt, kind):
        return nc.dram_tensor(name, list(shape), dt, kind=kind)

    xT = dram("xT", [NSEQ, 128, KC, S], F32, "ExternalInput")
    memT = dram("memT", [NSEQ, 128, KC, MEM], F32, "ExternalInput")
    wunits = dram("wunits", [NU, 128, USZ], F32, "ExternalInput")
    colsd = dram("cols", [128, NCOLS], F32, "ExternalInput")
    fbcd = dram("fbc", [128, NH], F32, "ExternalInput")
    constd = dram("consts", [128, CW], F32, "ExternalInput")
    wpd = dram("wp", [128, 14 * 128], F32, "ExternalInput")
    wfd = dram("wf", [128, KC * NH], F32, "ExternalInput")
    outT = dram("outT", [NSEQ, 128, KC, S], F32, "ExternalOutput")
    dbgT = dram("dbgT", [NSEQ, 128, KC, S], F32, "ExternalOutput") if debug else None
    wb = dram("wb", [NU, 128, USZ], BF16, "Internal")
    kcd = dram("kcache", [NSEQ, NH * 64, S], BF16, "Internal")
    vcd = dram("vcache", [NSEQ, S, NH * 64], BF16, "Internal")

    def sb(name, shape, dt):
        return stack.enter_context(nc.sbuf_tensor(name, list(shape), dt))

    wring = sb("wring", [128, RING, USZ], BF16)
    xr = sb("xr", [128, NSEQ, KC, T], F32)
    xb = sb("xb", [128, NSEQ, KC, T], BF16)
    scr = sb("scr", [128, NPC, T], BF16)
    utok = sb("utok", [128, 4, 768], BF16)
    uprev = sb("uprev", [128, NSEQ, 768], BF16)
    pT = sb("pT", [128, 4, T], BF16)
    rden = sb("rden", [128, 2, T], F32)
    tb = sb("tb", [128, 4, T], BF16)
    st = sb("st", [128, 4, T], F32)
    tf = sb("tf", [128, 2, T], F32)
    acc = sb("acc", [128, 4, T], F32)
    sg = sb("sg", [128, 2, T], BF16)
    chal = sb("chal", [128, NSEQ, 2, 44, 2], F32)
    hb = sb("hb", [128, 44, 2], F32)
    kt2 = sb("kt2", [128, 2, T], BF16)
    vt = sb("vt", [128, 2, 768], BF16)
    kst = sb("kst", [65, 3, 1024], BF16)
    vst = sb("vst", [128, 3, 8, 128], BF16)
    nFres = sb("nFres", [128, NSEQ, NKT, NH], F32)
    zt = sb("zt", [128, 4, NH], F32)
    lt = sb("lt", [128, 4, NH], F32)
    lb = sb("lb", [128, 4, NH], BF16)
    facc = sb("facc", [128, NSEQ, NH], F32)
    faccb = sb("faccb", [128, NSEQ, NH], BF16)
    ffm = sb("ffm", [NH, T], BF16)
    memKT = sb("memKT", [128, NSEQ, 2, 2, MEM], BF16)
    memV = sb("memV", [128, NSEQ, 2, 2, 4, 128], BF16)
    qmz = sb("qmz", [128, 4, T], BF16)
    memTb = sb("memTb", [128, KC, MEM], BF16)
    cols = sb("colsb", [128, NCOLS], F32)
    fbc = sb("fbcb", [128, NH], F32)
    cf = sb("cf", [128, 192], F32)
    cb = sb("cbf", [128, CW], BF16)
    wp = sb("wp_b", [128, 14 * 128], BF16)
    wf = sb("wf_b", [128, KC * NH], BF16)
    ones_b = sb("ones_b", [128, 128], BF16)
    ones_f = sb("ones_f", [128, 128], F32)
    epsc = sb("epsc", [128, 1], F32)
    onec = sb("onec", [128, 1], F32)
    ps = stack.enter_context(nc.psum_tensor("ps", [128, 8, T], F32))

    held = set()
    bank_ptr = [0]

    def nb():
        for _ in range(16):
            b = bank_ptr[0]
            bank_ptr[0] = (b + 1) % 8
            if b not in held:
                return b
        raise RuntimeError("no psum bank")

    def PS(b):
        return ("ps", b)

    def col(idx):
        return cols[:, idx:idx + 1]

    def pe_group(mms, reads, writes):
        def fn(e, mms=mms):
            ins = None
            for (o, l, r, s0, s1) in mms:
                ins = e.matmul(o, l, r, start=s0, stop=s1)
            return ins
        return sch.add("pe", fn, reads, writes)

    def act(out, in_, func, reads, writes, bias=None, scale=None):
        kw = {}
        if bias is not None:
            kw["bias"] = bias
        if scale is not None:
            kw["scale"] = scale
        return sch.add("act", lambda e: e.activation(out, in_, func, **kw), reads, writes)

    def dve(f, reads, writes):
        return sch.add("dve", f, reads, writes)

    def pool(f, reads, writes):
        return sch.add("pool", f, reads, writes)

    def dma(eng, out, in_, reads, writes, dsem):
        return sch.add(eng, lambda e: e.dma_start(out=out, in_=in_), reads, writes, dsem=dsem)

    order = []

    class Ring:
        def __init__(self):
            self.plan = []
            self.seen = set()
            self.issued = 0
            self.consumed = 0

        def set_plan(self, plan):
            self.plan = plan

        def _issue(self):
            k = self.issued
            name = self.plan[k]
            u = UIDX[name]
            n = KC * unit_ncols(name)
            slot = k % RING
            if name not in self.seen:
                self.seen.add(name)
                dma("pool", wring[:, slot, 0:n], wunits[u, :, 0:n], [], [("w", slot)], ("w", slot))
                dma("sp", wb[u, :, 0:n], wring[:, slot, 0:n], [("w", slot)], [("wb", u)], ("wbst", slot))
            else:
                dma("sp", wring[:, slot, 0:n], wb[u, :, 0:n], [("wb", u)], [("w", slot)], ("w", slot))
            self.issued += 1

        def get(self, name):
            k = self.consumed
            assert self.plan[k] == name, (self.plan[k], name)
            while self.issued < min(len(self.plan), k + RING - 1):
                self._issue()
            self.consumed += 1
            slot = k % RING
            return slot

    ring = Ring()
    plan = []
    for z in range(NSEQ):
        plan += ["MKV0", "MKV1"]
    M0U = ["AIN0", "AIN1", "AOUT0", "AOUT1"]
    F0U = ["UP0_%d" % i for i in range(11)] + ["DN0_%d_%d" % (hf, gi) for hf in range(2) for gi in range(3)]
    M1U = ["KV0", "KV1", "KV2", "BQ0", "BQ1", "BQM", "BOUT0", "BOUT1"]
    F1U = ["UP1_%d" % i for i in range(11)] + ["DN1_%d_%d" % (hf, gi) for hf in range(2) for gi in range(3)]
    for i in range(NT):
        for grp in (M0U, F0U, M1U, F1U):
            for z in range(NSEQ):
                plan += grp
    ring.set_plan(plan)

    def W(slot, ncols, kc, c0, c1, p0=0, p1=128):
        return wring[p0:p1, slot, kc * ncols + c0: kc * ncols + c1]

    dma("sp", cols[:, :], colsd[:, :], [], ["cols"], "i0")
    dma("sp", fbc[:, :], fbcd[:, :], [], ["fbc"], "i1")
    dma("sp", cf[:, 0:128], constd[:, 0:128], [], ["cf"], "i2")
    dma("sp", cf[:, 128:192], constd[:, C_INVC:C_INVC + 64], ["cf"], ["cf"], "i2b")
    dma("pool", cb[:, :], constd[:, :], [], ["cb"], "i3")
    dma("pool", wp[:, :], wpd[:, :], [], ["wp"], "i4")
    dma("pool", wf[:, :], wfd[:, :], [], ["wf"], "i5")
    pool(lambda e: e.memset(ones_b[:, :], 1.0), [], ["ones_b"])
    pool(lambda e: e.memset(ones_f[:, :], 1.0), [], ["ones_f"])
    pool(lambda e: e.memset(epsc[:, :], EPS), [], ["epsc"])
    pool(lambda e: e.memset(onec[:, :], 1.0), [], ["onec"])
    pool(lambda e: e.memset(kst[:, :, :], 1.0), [], [("kst", 0), ("kst", 1), ("kst", 2)])
    pool(lambda e: e.memset(vst[:, :, :, :], 1.0), [], [("vst", 0), ("vst", 1), ("vst", 2)])
    for z_ in range(NSEQ):
        for l_ in range(2):
            for m_ in range(2):
                pool(lambda e, z_=z_, l_=l_, m_=m_: e.memset(memV[:, z_, l_, m_, :, :], 1.0), [("memV", z_, l_)], [("memV", z_, l_)])
    pool(lambda e: e.memset(qmz[:, :, :], 0.0), [], [("qmz", h) for h in range(4)])

    def H(j, p1=128):
        return scr[0:p1, j, :], ("S", j)

    def MIX(c, p0=0, p1=128):
        return scr[p0:p1, c, :], ("S", c)

    def POOLED(c, p0=0, p1=128):
        return scr[p0:p1, 8 + c, :], ("S", 8 + c)

    def QAUG(h, c0=0, c1=T):
        return scr[0:65, 8 + h, c0:c1], ("S", 8 + h)

    def QM(j, p0=0, p1=128):
        return scr[p0:p1, 20 + j, :], ("S", 20 + j)

    cur = [0]

    def XRN(c, p=None):
        return ("xr", cur[0] if p is None else p, c)

    def XRV(c, p=None):
        return xr[:, cur[0] if p is None else p, c, :]

    def XBN(c):
        return ("xb", cur[0], c)

    def XBL():
        return [("xb", cur[0], c) for c in range(KC)]


    tbrot = Rot("tb", 4)
    tfrot = Rot("tf", 2)
    ln_state = {}

    late = []

    def drain_late(n=None, stream=None):
        saved = cur[0]
        if stream is not None:
            keep = []
            for (z, f) in list(late):
                if z == stream:
                    cur[0] = z
                    f()
                else:
                    keep.append((z, f))
            late[:] = keep
        else:
            k = len(late) if n is None else min(n, len(late))
            for _ in range(k):
                z, f = late.pop(0)
                cur[0] = z
                f()
        cur[0] = saved

    def ln_begin(drain=True):
        if drain:
            drain_late()
        ln_state["b"] = None
        ln_state["n"] = 0

    def ln_banks():
        if ln_state["b"] is None:
            b1 = nb()
            held.add(b1)
            b2 = nb()
            held.add(b2)
            ln_state["b"] = (b1, b2)
        return ln_state["b"]

    def res_chunk(oc, bank):
        xv, xn = XRV(oc), XRN(oc)
        dve(lambda e: e.scalar_tensor_tensor(out=xv, in0=xv, scalar=ALPHA,
                                             in1=ps[:, bank, :], op0=ALU.mult, op1=ALU.add),
            [PS(bank), xn], [xn])
        i1, n1 = tbrot.next()
        act(tb[:, i1, :], xv, AF.Copy, [xn], [n1])
        i2, n2 = tbrot.next()
        act(tb[:, i2, :], xv, AF.Square, [xn], [n2])
        ln_state.setdefault("pend", []).append((i1, n1, i2, n2))

    def stat_flush(keep=0):
        pend = ln_state.setdefault("pend", [])
        if len(pend) <= keep:
            return
        b1, b2 = ln_banks()
        while len(pend) > keep:
            i1, n1, i2, n2 = pend.pop(0)
            k = ln_state["n"]
            ln_state["n"] += 1
            pe_group([(ps[:, b1, :], ones_b[:, :], tb[:, i1, :], k == 0, k == KC - 1),
                      (ps[:, b2, :], ones_b[:, :], tb[:, i2, :], k == 0, k == KC - 1)],
                     [n1, n2, "ones_b"], [PS(b1), PS(b2)])

    def ln_finish(gkey, bkey, l, write_xb=True):
        stat_flush(0)
        b1, b2 = ln_banks()
        mean, var, rstd, nmr = st[:, 0, :], st[:, 1, :], st[:, 2, :], st[:, 3, :]
        act(mean, ps[:, b1, :], AF.Copy, [PS(b1)], [("st", 0)], scale=1.0 / D)
        dve(lambda e: e.tensor_tensor(out=var, in0=mean, in1=mean, op=ALU.mult), [("st", 0)], [("st", 1)])
        dve(lambda e: e.scalar_tensor_tensor(out=var, in0=ps[:, b2, :], scalar=1.0 / D, in1=var,
                                             op0=ALU.mult, op1=ALU.subtract),
            [PS(b2), ("st", 1)], [("st", 1)])
        held.discard(b1)
        held.discard(b2)
        z0 = cur[0]

        def tail_head():
            act(rstd, var, AF.Ln, [("st", 1), "epsc"], [("st", 2)], bias=epsc[:, 0:1], scale=1.0)
            act(rstd, rstd, AF.Exp, [("st", 2)], [("st", 2)], scale=-0.5)
            dve(lambda e: e.scalar_tensor_tensor(out=nmr, in0=mean, scalar=-1.0, in1=rstd,
                                                 op0=ALU.mult, op1=ALU.mult),
                [("st", 0), ("st", 2)], [("st", 3)])
        late.append((z0, tail_head))
        g0 = COLMAP[(gkey, l)]
        b0 = COLMAP[(bkey, l)]

        def chunk(c):
            i, n = tfrot.next()
            xv, xn = XRV(c), XRN(c)
            dve(lambda e, i=i, xv=xv: e.tensor_tensor(out=tf[:, i, :], in0=xv, in1=rstd, op=ALU.mult),
                [xn, ("st", 2)], [n])
            dve(lambda e, i=i: e.tensor_tensor(out=tf[:, i, :], in0=tf[:, i, :], in1=nmr, op=ALU.add),
                [n, ("st", 3)], [n])
            if write_xb:
                act(xb[:, cur[0], c, :], tf[:, i, :], AF.Identity, [n, "cols"], [XBN(c)],
                    bias=col(b0 + c), scale=col(g0 + c))
            act(xv, tf[:, i, :], AF.Identity, [n, "cols"], [xn],
                bias=col(b0 + c), scale=col(g0 + c))
        for c in range(KC):
            late.append((z0, (lambda c=c: chunk(c))))

    ptrot = Rot("pT", 4)
    rdrot = Rot("rden", 2)

    def qm_evac(j, bank):
        act(qmz[0:64, 2 * j, :], ps[0:64, bank, :], AF.Copy, [PS(bank)], [("qmz", 2 * j)])
        act(qmz[64:128, 2 * j + 1, :], ps[64:128, bank, :], AF.Copy, [PS(bank)], [("qmz", 2 * j + 1)])

    def mem_attention(l):
        steps = [(h, mc) for h in range(4) for mc in range(2)]
        stt = {}

        def qk(h, mc):
            if mc == 0:
                bh = nb()
                held.add(bh)
                stt[("b", h)] = bh
            bl = nb()
            pe_group([(ps[:, bl, :], memKT[:, cur[0], l, h // 2, mc * 128:(mc + 1) * 128], qmz[:, h, :], True, True)],
                     [("qmz", h), ("memKT", cur[0], l)], [PS(bl)])
            pi, pn = ptrot.next()
            act(pT[:, pi, :], ps[:, bl, :], AF.Exp, [PS(bl)], [pn], scale=0.125)
            stt[(h, mc)] = (pi, pn)

        def pv(h, mc):
            pi, pn = stt.pop((h, mc))
            bh = stt[("b", h)]
            pe_group([(ps[:, bh, :], memV[:, cur[0], l, mc, h, :], pT[:, pi, :], mc == 0, mc == 1)],
                     [pn, ("memV", cur[0], l)], [PS(bh)])
            if mc == 1:
                hp = 64 * (h % 2)
                ri, rn = rdrot.next()
                act(rden[0:64, ri, :], ps[64:128, bh, :], AF.Ln, [PS(bh)], [rn])
                act(rden[0:64, ri, :], rden[0:64, ri, :], AF.Exp, [rn], [rn], scale=-1.0)
                mv, mn = MIX(6 + h // 2, hp, hp + 64)
                dve(lambda e, ri=ri, bh=bh, mv=mv: e.tensor_tensor(out=mv, in0=ps[0:64, bh, :], in1=rden[0:64, ri, :], op=ALU.mult),
                    [PS(bh), rn, mn], [mn])
                held.discard(bh)

        SK = 2
        for n in range(len(steps) + SK):
            if n < len(steps):
                qk(*steps[n])
            if n - SK >= 0:
                pv(*steps[n - SK])

    def wout_ln(unames, l):
        ln_begin()
        for oc in range(KC):
            if oc % 4 == 0:
                slot = ring.get(unames[oc // 4])
            b = nb()
            mms = []
            for kc in range(KC):
                mv, mn = MIX(kc)
                mms.append((ps[:, b, :], W(slot, 512, kc, (oc % 4) * 128, (oc % 4) * 128 + 128), mv, kc == 0, kc == KC - 1))
            pe_group(mms, [("w", slot)] + [("S", c) for c in range(KC)], [PS(b)])
            res_chunk(oc, b)
            stat_flush(1)
        ln_finish("ln1g", "ln1b", l)

    accrot = Rot("acc", 4)
    sgrot = Rot("sg", 2)

    def ffn(l, after_up=None, after_lnbegin=None):
        cw = [COLMAP[("cw", l, k)] for k in range(3)]
        cbi = COLMAP[("cb", l)]
        halr = [("chal", cur[0], l, ci) for ci in range(44)]
        h0, h1 = chal[:, cur[0], l, :, 0], chal[:, cur[0], l, :, 1]
        pool(lambda e: e.tensor_tensor(out=hb[:, :, 0], in0=h1, in1=cols[:, cw[1]:cw[1] + 44], op=ALU.mult), halr + ["cols"], ["hb"])
        pool(lambda e: e.tensor_tensor(out=hb[:, :, 1], in0=h0, in1=cols[:, cw[0]:cw[0] + 44], op=ALU.mult), halr + ["cols", "hb"], ["hb"])
        pool(lambda e: e.tensor_tensor(out=hb[:, :, 0], in0=hb[:, :, 0], in1=hb[:, :, 1], op=ALU.add), ["hb"], ["hb"])
        pool(lambda e: e.tensor_tensor(out=hb[:, :, 1], in0=h1, in1=cols[:, cw[0]:cw[0] + 44], op=ALU.mult), halr + ["cols", "hb"], ["hb"])
        gate_pending = []
        for i in range(11):
            slot = ring.get("UP%d_%d" % (l, i))
            for jj in range(2):
                j = 2 * i + jj
                M = 128
                br = []
                for ug in range(2):
                    ci = ug * 22 + j
                    b = nb()
                    mms = []
                    for kc in range(KC):
                        mms.append((ps[0:M, b, :], wring[:, slot, kc * 512 + ug * 256 + jj * 128: kc * 512 + ug * 256 + jj * 128 + M],
                                    xb[:, cur[0], kc, :], kc == 0, kc == KC - 1))
                    pe_group(mms, [("w", slot)] + XBL(), [PS(b)])
                    ai, an = accrot.next()
                    br.append((ci, b, ai, an))
                for (ci, b, ai, an) in br:
                    act(chal[0:M, cur[0], l, ci, 0:2], ps[0:M, b, T - 2:T], AF.Copy, [PS(b)], [("chal", cur[0], l, ci)])
                    act(acc[0:M, ai, :], ps[0:M, b, :], AF.Identity, [PS(b), "cols"], [an],
                        bias=cols[0:M, cbi + ci:cbi + ci + 1], scale=cols[0:M, cw[2] + ci:cw[2] + ci + 1])
                for (ci, b, ai, an) in br:
                    w1 = cols[0:M, cw[1] + ci:cw[1] + ci + 1]
                    dve(lambda e, ai=ai, b=b, M=M, w1=w1: e.scalar_tensor_tensor(
                        out=acc[0:M, ai, 1:T], in0=ps[0:M, b, 0:T - 1], scalar=w1, in1=acc[0:M, ai, 1:T],
                        op0=ALU.mult, op1=ALU.add), [PS(b), an, "cols"], [an])
                for (ci, b, ai, an) in br:
                    w0 = cols[0:M, cw[0] + ci:cw[0] + ci + 1]
                    dve(lambda e, ai=ai, b=b, M=M, w0=w0: e.scalar_tensor_tensor(
                        out=acc[0:M, ai, 2:T], in0=ps[0:M, b, 0:T - 2], scalar=w0, in1=acc[0:M, ai, 2:T],
                        op0=ALU.mult, op1=ALU.add), [PS(b), an, "cols"], [an])
                for (ci, b, ai, an) in br:
                    dve(lambda e, ai=ai, M=M, ci=ci: e.tensor_tensor(out=acc[0:M, ai, 0:2], in0=acc[0:M, ai, 0:2],
                                                                     in1=hb[0:M, ci, :], op=ALU.add), ["hb", an], [an])
                if gate_pending:
                    gate_pending.pop(0)()

                def gate(br=br, j=j, M=M):
                    (_, _, au, aun), (_, _, ag, agn) = br
                    si, sn = sgrot.next()
                    act(sg[0:M, si, :], acc[0:M, ag, :], AF.Silu, [agn], [sn])
                    hv, hn2 = H(j, M)
                    pool(lambda e, hv=hv, si=si, au=au, M=M: e.tensor_tensor(out=hv, in0=sg[0:M, si, :], in1=acc[0:M, au, :], op=ALU.mult),
                         [sn, aun], [hn2])
                gate_pending.append(gate)
        while gate_pending:
            gate_pending.pop(0)()
        if after_up is not None:
            after_up()
        ln_begin(drain=(after_lnbegin is not None))
        if after_lnbegin is not None:
            after_lnbegin()
        for hf in range(2):
            banks = []
            for o in range(4):
                b = nb()
                held.add(b)
                banks.append(b)
            for gi in range(3):
                slot = ring.get("DN%d_%d_%d" % (l, hf, gi))
                mms = []
                reads = [("w", slot)]
                for jl in range(8):
                    j = 8 * gi + jl
                    if j >= NPC:
                        break
                    K = 128
                    reads.append(("S", j))
                    for o in range(4):
                        mms.append((ps[:, banks[o], :], wring[0:K, slot, jl * 512 + o * 128: jl * 512 + o * 128 + 128],
                                    scr[0:K, j, :], j == 0, j == NPC - 1))
                pe_group(mms, reads, [PS(b) for b in banks])
            for o in range(4):
                held.discard(banks[o])
                res_chunk(4 * hf + o, banks[o])
                stat_flush(1)
            if hf == 0:
                drain_late()
        ln_finish("ln2g", "ln2b", l, write_xb=(l == 0))


    def seq_setup(s):
        pool(lambda e: e.memset(chal[:, s, :, :, :], 0.0), [], [("chal", s, l, ci) for l in range(2) for ci in range(44)])
        pool(lambda e: e.memset(facc[:, s, :], 0.0), [], [("facc", s)])
        pool(lambda e: e.memset(faccb[:, s, :], 0.0), [], [("faccb", s)])
        dma("pool", memTb[:, :, :], memT[s, :, :, :], [], [("memTb", kc) for kc in range(KC)], "memTb")
        for l in range(2):
            slot = ring.get("MKV%d" % l)
            for j in range(2):
                b = nb()
                mms = [(ps[:, b, 0:MEM], W(slot, 512, kc, j * 128, j * 128 + 128), memTb[:, kc, :], kc == 0, kc == KC - 1)
                       for kc in range(KC)]
                pe_group(mms, [("w", slot)] + [("memTb", kc) for kc in range(KC)], [PS(b)])
                act(memKT[:, s, l, j, :], ps[:, b, 0:MEM], AF.Copy, [PS(b)], [("memKT", s, l)])
            for mc in range(2):
                b = nb()
                mms = [(ps[:, b, 0:MEM], memTb[:, kc, mc * 128:(mc + 1) * 128], W(slot, 512, kc, 256, 512), kc == 0, kc == KC - 1)
                       for kc in range(KC)]
                pe_group(mms, [("w", slot)] + [("memTb", kc) for kc in range(KC)], [PS(b)])
                for h in range(4):
                    act(memV[:, s, l, mc, h, 0:64], ps[:, b, 64 * h:64 * h + 64], AF.Copy, [PS(b), ("memV", s, l)], [("memV", s, l)])

    WIN = {0: [(0, 128, 2)], 1: [(0, 64, 2), (64, 128, 4)], 2: [(0, 128, 4)], 3: [(0, 128, 8)],
           4: [(0, 64, 8), (64, 128, 16)], 5: [(0, 128, 16)]}

    def layer0_mixer(first):
        z = cur[0]
        s_in = [ring.get("AIN0"), ring.get("AIN1")]
        for sub in range(4):
            b1 = nb()
            b2 = nb()
            tk = xb[:, z, :, sub * 128:(sub + 1) * 128]
            mms = [(ps[:, b1, :], xb[:, z, kc, sub * 128:(sub + 1) * 128], W(s_in[0], 512, kc, 0, 512), kc == 0, kc == KC - 1)
                   for kc in range(KC)]
            mms += [(ps[:, b2, 0:256], xb[:, z, kc, sub * 128:(sub + 1) * 128], W(s_in[1], 512, kc, 0, 256), kc == 0, kc == KC - 1)
                    for kc in range(KC)]
            pe_group(mms, [("w", s_in[0]), ("w", s_in[1])] + XBL(), [PS(b1), PS(b2)])
            drain_late(1)
            act(utok[:, sub, 0:512], ps[:, b1, :], AF.Copy, [PS(b1)], [("utok", sub)])
            act(utok[:, sub, 512:768], ps[:, b2, 0:256], AF.Copy, [PS(b2), ("utok", sub)], [("utok", sub)])
        for oc in (6, 7):
            b = nb()
            mms = [(ps[:, b, :], W(s_in[1], 512, kc, (oc % 4) * 128, (oc % 4) * 128 + 128), xb[:, z, kc, :], kc == 0, kc == KC - 1)
                   for kc in range(KC)]
            pe_group(mms, [("w", s_in[1])] + XBL(), [PS(b)])
            drain_late(1)
            qm_evac(oc - 6, b)
        for c in range(6):
            for (p0, p1, w) in WIN[c]:
                wi = {2: 0, 4: 1, 8: 2, 16: 3}[w]
                bmain = cb[:, C_B + (wi * 3 + 0) * 128:C_B + (wi * 3 + 0) * 128 + 128]
                bhalo = cb[:, C_B + (wi * 3 + 1) * 128:C_B + (wi * 3 + 1) * 128 + 128]
                bfirst = cb[:, C_B + (wi * 3 + 2) * 128:C_B + (wi * 3 + 2) * 128 + 128]
                bv = nb()
                mms = []
                reads = ["cb"] + [("utok", sub) for sub in range(4)]
                for sub in range(4):
                    o = ps[:, bv, sub * 128:(sub + 1) * 128]
                    cur_u = utok[:, sub, c * 128:(c + 1) * 128]
                    if sub == 0 and first:
                        mms.append((o, cur_u, bfirst, True, True))
                        continue
                    mms.append((o, cur_u, bmain, True, False))
                    if sub == 0:
                        mms.append((o, uprev[:, z, c * 128:(c + 1) * 128], bhalo, False, True))
                        reads.append(("uprev", z))
                    else:
                        mms.append((o, utok[:, sub - 1, c * 128:(c + 1) * 128], bhalo, False, True))
                pe_group(mms, reads, [PS(bv)])
                pv, pn = POOLED(c, p0, p1)
                act(pv, ps[p0:p1, bv, :], AF.Copy, [PS(bv), pn], [pn])
        dve(lambda e, z=z: e.tensor_copy(uprev[:, z, :], utok[:, 3, :]), [("utok", 3)], [("uprev", z)])

    GL_TILES = {0: [0, 1], 1: [0, 1, 2], 2: [1, 2], 3: [3, 4], 4: [3, 4, 5], 5: [4, 5]}

    def layer0_glinear():
        PSC = COLMAP["pscale"]
        ti = 0
        for oc in range(6):
            b = nb()
            mms = []
            reads = ["wp"]
            srcs = GL_TILES[oc]
            for k, ch in enumerate(srcs):
                pv, pn = POOLED(ch)
                reads.append(pn)
                mms.append((ps[:, b, :], wp[:, ti * 128:(ti + 1) * 128], pv, k == 0, k == len(srcs) - 1))
                ti += 1
            pe_group(mms, reads, [PS(b)])
            mv, mn = MIX(oc)
            act(mv, ps[:, b, :], AF.Identity, [PS(b), "cols"], [mn], scale=col(PSC + oc))

    ktrot = Rot("kt2", 2)
    vtrot = Rot("vt", 2)
    ztrot = Rot("zt", 4)

    def layer1_proj(s, i):
        t0 = i * T
        bfm = nb()
        held.add(bfm)
        fst = {}

        def stageA(sub):
            b = nb()
            mms = [(ps[:, b, 0:NH], xb[:, cur[0], kc, sub * 128:(sub + 1) * 128], wf[:, kc * NH:(kc + 1) * NH], kc == 0, kc == KC - 1)
                   for kc in range(KC)]
            pe_group(mms, ["wf"] + XBL(), [PS(b)])
            zi, zn = ztrot.next()
            ltn, lbn = ("lt", zi), ("lb", zi)
            dve(lambda e, zi=zi, b=b: e.tensor_tensor(out=zt[:, zi, :], in0=ps[:, b, 0:NH], in1=fbc[:, :], op=ALU.add),
                [PS(b), "fbc"], [zn])
            act(zt[:, zi, :], zt[:, zi, :], AF.Exp, [zn], [zn], scale=-1.0)
            act(lt[:, zi, :], zt[:, zi, :], AF.Ln, [zn, "onec"], [ltn], bias=onec[:, 0:1], scale=1.0)
            dve(lambda e, zi=zi: e.tensor_copy(lb[:, zi, :], lt[:, zi, :]), [ltn], [lbn])
            fst[sub] = (zi, ltn, lbn)

        def stageB(sub):
            zi, ltn, lbn = fst[sub]
            gsub = 4 * i + sub
            b2 = nb()
            pe_group([(ps[:, b2, 0:NH], cf[:, C_U:C_U + 128], lt[:, zi, :], True, False),
                      (ps[:, b2, 0:NH], ones_f[:, :], facc[:, cur[0], :], False, True)],
                     ["cf", ltn, "ones_f", ("facc", cur[0])], [PS(b2)])
            pe_group([(ps[0:NH, bfm, sub * 128:(sub + 1) * 128], lb[:, zi, :], cb[:, C_U:C_U + 128], True, False),
                      (ps[0:NH, bfm, sub * 128:(sub + 1) * 128], faccb[:, cur[0], :], ones_b[:, :], False, True)],
                     [lbn, "cb", ("faccb", cur[0]), "ones_b"], [PS(bfm)])
            dve(lambda e, b2=b2, gsub=gsub, z=cur[0]: e.tensor_copy(nFres[:, z, gsub, :], ps[:, b2, 0:NH]), [PS(b2)], [("nF", cur[0], gsub)])
            dve(lambda e, zi=zi, z=cur[0]: e.tensor_tensor(out=facc[:, z, :], in0=facc[:, z, :], in1=lt[:, zi, :], op=ALU.add),
                [ltn, ("facc", cur[0])], [("facc", cur[0])])
            dve(lambda e, z=cur[0]: e.tensor_copy(faccb[:, z, :], facc[:, z, :]), [("facc", cur[0])], [("faccb", cur[0])])

        s0 = ring.get("KV0")
        s1 = ring.get("KV1")

        def kgroup(j):
            slot, cofs = (s0, 128 * j) if j < 4 else (s1, 128 * (j - 4))
            b = nb()
            mms = [(ps[:, b, :], W(slot, 512, kc, cofs, cofs + 128), xb[:, cur[0], kc, :], kc == 0, kc == KC - 1) for kc in range(KC)]
            pe_group(mms, [("w", slot)] + XBL(), [PS(b)])
            drain_late(1)
            r, rn = ktrot.next()
            act(kt2[:, r, :], ps[:, b, :], AF.Copy, [PS(b)], [rn])
            dma("sp", kcd[s, 128 * j:128 * j + 128, t0:t0 + T], kt2[:, r, :], [rn], [("kc", s, j, i)], rn)

        def vgroup(sub, s2):
            b1 = nb()
            b2 = nb()
            mms = [(ps[:, b1, 0:256], xb[:, cur[0], kc, sub * 128:(sub + 1) * 128], W(s1, 512, kc, 256, 512), kc == 0, kc == KC - 1)
                   for kc in range(KC)]
            mms += [(ps[:, b2, :], xb[:, cur[0], kc, sub * 128:(sub + 1) * 128], W(s2, 512, kc, 0, 512), kc == 0, kc == KC - 1)
                    for kc in range(KC)]
            pe_group(mms, [("w", s1), ("w", s2)] + XBL(), [PS(b1), PS(b2)])
            drain_late(1)
            r, rn = vtrot.next()
            act(vt[:, r, 0:256], ps[:, b1, 0:256], AF.Copy, [PS(b1)], [rn])
            act(vt[:, r, 256:768], ps[:, b2, :], AF.Copy, [PS(b2), rn], [rn])
            dma("sp", vcd[s, t0 + sub * 128:t0 + sub * 128 + 128, :], vt[:, r, :], [rn], [("vc", s, i, sub)], rn)

        for sub in range(4):
            stageA(sub)
        kgroup(0)
        kgroup(1)
        stageB(0)
        kgroup(2)
        kgroup(3)
        stageB(1)
        kgroup(4)
        kgroup(5)
        stageB(2)
        s2 = ring.get("KV2")
        vgroup(0, s2)
        vgroup(1, s2)
        stageB(3)
        act(ffm[:, :], ps[0:NH, bfm, :], AF.Copy, [PS(bfm)], ["ffm"], scale=-1.0)
        held.discard(bfm)
        vgroup(2, s2)
        vgroup(3, s2)
        for h in range(NH):
            if h % 6 == 0:
                slot = ring.get("BQ%d" % (h // 6))
            b = nb()
            hc = (h % 6) * 65
            mms = [(ps[0:65, b, :], W(slot, 390, kc, hc, hc + 65), xb[:, cur[0], kc, :], kc == 0, False) for kc in range(KC)]
            mms.append((ps[0:65, b, :], cb[0:NH, C_E + 65 * h:C_E + 65 * h + 65], ffm[:, :], False, True))
            qv, qn = QAUG(h)
            pe_group(mms, [("w", slot), "cb", "ffm"] + XBL(), [PS(b)])
            act(qv, ps[0:65, b, :], AF.Copy, [PS(b)], [qn], scale=0.125)
        slot = ring.get("BQM")
        for j in range(2):
            b = nb()
            mms = [(ps[:, b, :], W(slot, 256, kc, j * 128, j * 128 + 128), xb[:, cur[0], kc, :], kc == 0, kc == KC - 1) for kc in range(KC)]
            pe_group(mms, [("w", slot)] + XBL(), [PS(b)])
            qm_evac(j, b)

    ksrot = Rot("kst", 3)
    vsrot = Rot("vst", 3)

    def fox_prefetch(s, i):
        nkt = 4 * (i + 1)
        n_here = min(8, nkt)
        nk = 128 * n_here
        ki, kn = ksrot.next()
        vi, vn = vsrot.next()
        kreads = [("kc", s, 0, ti) for ti in range(0, (nk - 1) // T + 1)]
        vreads = [("vc", s, a_ // 4, a_ % 4) for a_ in range(n_here)]
        dma("sp", kst[0:64, ki, 0:nk], kcd[s, 0:64, 0:nk], kreads, [kn], kn)
        vsrc = vcd.rearrange("s (kt p) d -> s p kt d", p=128)[s, :, 0:n_here, 0:64]
        dma("sp", vst[:, vi, 0:n_here, 0:64], vsrc, vreads, [vn], vn)
        return {(0, 0): (ki, kn, vi, vn)}

    def fox_attention(s, i, pre=None):
        nkt = 4 * (i + 1)
        for b in (6, 7):
            held.add(b)
        steps = []
        for h in range(NH):
            hp = 64 * (h % 2)
            ba = 6 + (h % 2)
            nch = (nkt + 7) // 8
            for ck in range(nch):
                kt0 = 8 * ck
                n_here = min(8, nkt - kt0)
                for kk in range(n_here):
                    steps.append((h, hp, ba, ck, kt0, n_here, kk))
        state = dict(pre or {})

        def qk(st_):
            h, hp, ba, ck, kt0, n_here, kk = st_
            if kk == 0 and (h, ck) not in state:
                nk = 128 * n_here
                k0 = 128 * kt0
                ki, kn = ksrot.next()
                vi, vn = vsrot.next()
                kreads = [("kc", s, h // 2, ti) for ti in range(k0 // T, (k0 + nk - 1) // T + 1)]
                vreads = [("vc", s, (kt0 + a_) // 4, (kt0 + a_) % 4) for a_ in range(n_here)]
                dma("sp", kst[0:64, ki, 0:nk], kcd[s, 64 * h:64 * h + 64, k0:k0 + nk], kreads, [kn], kn)
                vsrc = vcd.rearrange("s (kt p) d -> s p kt d", p=128)[s, :, kt0:kt0 + n_here, 64 * h:64 * h + 64]
                dma("sp", vst[:, vi, 0:n_here, 0:64], vsrc, vreads, [vn], vn)
                state[(h, ck)] = (ki, kn, vi, vn)
            ki, kn, vi, vn = state[(h, ck)]
            kt = kt0 + kk
            jj = kt - 4 * i
            q0 = 128 * jj if jj >= 0 else 0
            N = T - q0
            bl = nb()
            qv, qn = QAUG(h, q0, T)
            mms = [(ps[:, bl, 0:N], kst[0:65, ki, kk * 128:(kk + 1) * 128], qv, True, jj < 0)]
            reads = [kn, qn]
            if jj >= 0:
                mms.append((ps[:, bl, 0:128], cb[:, C_ID:C_ID + 128], cb[:, C_MASK:C_MASK + 128], False, True))
                reads.append("cb")
            pe_group(mms, reads, [PS(bl)])
            pi, pn = ptrot.next()
            act(pT[:, pi, 0:N], ps[:, bl, 0:N], AF.Exp, [PS(bl), ("nF", cur[0], kt)], [pn],
                bias=nFres[:, cur[0], kt, h:h + 1], scale=1.0)
            state[st_] = (pi, pn, q0, N, kt, vi, vn)

        def pv(st_):
            h, hp, ba, ck, kt0, n_here, kk = st_
            pi, pn, q0, N, kt, vi, vn = state.pop(st_)
            pe_group([(ps[:, ba, q0:T], vst[:, vi, kk, :], pT[:, pi, 0:N], kt == 0, kt == nkt - 1)],
                     [vn, pn], [PS(ba)])
            if kt == nkt - 1:
                ri, rn = rdrot.next()
                if h == NH - 1:
                    act(rden[0:64, ri, :], ps[64:128, ba, :], AF.Ln, [PS(ba)], [rn])
                    act(rden[0:64, ri, :], rden[0:64, ri, :], AF.Exp, [rn], [rn], scale=-1.0)
                else:
                    dve(lambda e, ri=ri, ba=ba: e.reciprocal(rden[0:64, ri, :], ps[64:128, ba, :]), [PS(ba)], [rn])
                mv, mn = MIX(h // 2, hp, hp + 64)
                dve(lambda e, ri=ri, ba=ba, mv=mv: e.tensor_tensor(out=mv, in0=ps[0:64, ba, :],
                                                                  in1=rden[0:64, ri, :], op=ALU.mult),
                    [PS(ba), rn, mn], [mn])

        SK = 2
        for n in range(len(steps) + SK):
            if n < len(steps):
                qk(steps[n])
            if n - SK >= 0:
                pv(steps[n - SK])
            if n % 3 == 0:
                drain_late(1)
        for b in (6, 7):
            held.discard(b)

    def load_x(z, i):
        dma("pool", xr[:, z, :, :], xT[z, :, :, i * T:i * T + T], [], [("xr", z, c) for c in range(KC)], ("x", z))

    def cast_x(z):
        for c in range(0, KC, 2):
            dve(lambda e, c=c, z=z: e.tensor_copy(xb[:, z, c:c + 2, :], xr[:, z, c:c + 2, :]),
                [("xr", z, c), ("xr", z, c + 1)], [("xb", z, c), ("xb", z, c + 1)])

    need_cast = set()

    def finish_tile(z, i):
        drain_late(stream=z)
        dma("pool", outT[z, :, :, i * T:i * T + T], xr[:, z, :, :], [("xr", z, c) for c in range(KC)], [], ("out", z))
        if i + 1 < NT:
            load_x(z, i + 1)
            need_cast.add(z)

    for z in range(NSEQ):
        load_x(z, 0)
    for z in range(NSEQ):
        cur[0] = z
        seq_setup(z)
        cast_x(z)
    for i in range(NT):
        t0 = i * T
        for z in range(NSEQ):
            cur[0] = z
            drain_late(stream=z)
            if z in need_cast:
                need_cast.discard(z)
                cast_x(z)
            layer0_mixer(first=(i == 0))
            if NSEQ == 2 and z == 0 and i > 0:
                finish_tile(1, i - 1)
                cur[0] = z
            mem_attention(0)
            layer0_glinear()
            wout_ln(["AOUT0", "AOUT1"], 0)
        for z in range(NSEQ):
            cur[0] = z
            drain_late(stream=z)
            ffn(0)
            if debug:
                drain_late(stream=z)
            if debug:
                dma("sp", dbgT[z, :, :, t0:t0 + T], xr[:, z, :, :], [XRN(c) for c in range(KC)], [], ("dbg", z))
        for z in range(NSEQ):
            cur[0] = z
            drain_late(stream=z)
            layer1_proj(z, i)
            pre = fox_prefetch(z, i)
            mem_attention(1)
            fox_attention(z, i, pre)
            wout_ln(["BOUT0", "BOUT1"], 1)
        for z in range(NSEQ):
            cur[0] = z
            drain_late(stream=z)
            if NSEQ == 2 and z == 1:
                def hook(i=i):
                    finish_tile(0, i)
                    cur[0] = 1
                ffn(1, after_lnbegin=hook)
            else:
                ffn(1)
            if NSEQ == 1:
                finish_tile(z, i)
    if NSEQ == 2:
        finish_tile(1, NT - 1)

    drain_late()
    assert ring.consumed == len(plan), (ring.consumed, len(plan))
    sch.emit(nc, stack)
    stack.close()
    return nc, sch


def _unit(Wsub):
    n = Wsub.shape[1]
    a = np.ascontiguousarray(Wsub.reshape(KC, 128, n).transpose(1, 0, 2)).reshape(128, KC * n)
    out = np.zeros((128, USZ), np.float32)
    out[:, :KC * n] = a
    return out


def _host_prepare(inp):
    f = np.float32
    units = np.zeros((NU, 128, USZ), f)
    a_w_in, a_w_out = inp["a_w_in"][0], inp["a_w_out"][0]
    b_w_q, b_w_out = inp["b_w_q"][0], inp["b_w_out"][0]
    kv_w = inp["kv_w"]
    for hfi in range(2):
        units[UIDX["AIN%d" % hfi]] = _unit(a_w_in[:, 512 * hfi:512 * hfi + 512])
        units[UIDX["AOUT%d" % hfi]] = _unit(a_w_out[:, 512 * hfi:512 * hfi + 512])
        units[UIDX["BOUT%d" % hfi]] = _unit(b_w_out[:, 512 * hfi:512 * hfi + 512])
    for l in range(2):
        Wup = inp["ffn_w_up"][l]
        Wd = inp["ffn_w_down"][l]
        for i in range(11):
            blk = np.zeros((D, 2, 256), f)
            c0, c1 = 256 * i, min(256 * i + 256, DFF)
            blk[:, 0, :c1 - c0] = Wup[:, c0:c1]
            blk[:, 1, :c1 - c0] = Wup[:, DFF + c0:DFF + c1]
            units[UIDX["UP%d_%d" % (l, i)]] = _unit(blk.reshape(D, 512))
        Wdp = np.zeros((NPC * 128, D), f)
        Wdp[:DFF] = Wd
        Wdp = Wdp.reshape(NPC, 128, D)
        for hf in range(2):
            for gi in range(3):
                blk = np.zeros((8, 128, 512), f)
                n = min(8, NPC - 8 * gi)
                blk[:n] = Wdp[8 * gi:8 * gi + n, :, 512 * hf:512 * hf + 512]
                units[UIDX["DN%d_%d_%d" % (l, hf, gi)]] = np.ascontiguousarray(blk.transpose(1, 0, 2)).reshape(128, USZ)
        units[UIDX["MKV%d" % l]] = _unit(inp["mem_w_kv"][l])
    for k in range(3):
        units[UIDX["KV%d" % k]] = _unit(kv_w[:, 512 * k:512 * k + 512])
    for k in range(2):
        blk = np.zeros((D, 6, 65), f)
        blk[:, :, :64] = b_w_q[:, 384 * k:384 * k + 384].reshape(D, 6, 64)
        units[UIDX["BQ%d" % k]] = _unit(blk.reshape(D, 390))
    units[UIDX["BQM"]] = _unit(b_w_q[:, 768:1024])

    cols = np.zeros((128, NCOLS), f)
    for l in range(2):
        for nm, key in (("ln1g", "ln1_g"), ("ln1b", "ln1_b"), ("ln2g", "ln2_g"), ("ln2b", "ln2_b")):
            cols[:, COLMAP[(nm, l)]:COLMAP[(nm, l)] + 8] = inp[key][l].reshape(8, 128).T
        cwp = np.zeros((3, 2, NPC * 128), f)
        cwp[:, :, :DFF] = inp["ffn_conv_w"][l].reshape(3, 2, DFF)
        cbp = np.zeros((2, NPC * 128), f)
        cbp[:, :DFF] = inp["ffn_conv_b"][l].reshape(2, DFF)
        for k in range(3):
            cols[:, COLMAP[("cw", l, k)]:COLMAP[("cw", l, k)] + 44] = cwp[k].reshape(44, 128).T
        cols[:, COLMAP[("cb", l)]:COLMAP[("cb", l)] + 44] = cbp.reshape(44, 128).T
    cols[:, COLMAP["pscale"]:COLMAP["pscale"] + 6] = inp["a_pool_scale"][0].reshape(6, 128).T

    fbc = np.ascontiguousarray(np.broadcast_to(inp["f_b"].astype(f)[None, :], (128, NH)))

    consts = np.zeros((128, CW), f)
    ii = np.arange(128)
    consts[:, C_U:C_U + 128] = (ii[:, None] <= ii[None, :]).astype(f)
    consts[:, C_ID:C_ID + 128] = np.eye(128, dtype=f)
    consts[:, C_MASK:C_MASK + 128] = np.where(ii[:, None] > ii[None, :], NEG, 0.0).astype(f)
    for wi, w in enumerate((2, 4, 8, 16)):
        consts[:, C_INVC + 16 * wi:C_INVC + 16 * wi + 16] = (1.0 / np.minimum(np.arange(16) + 1, w))[None, :]
    for h in range(NH):
        consts[h, C_E + 65 * h + 64] = 8.0
    ss, tt = ii[:, None], ii[None, :]
    for wi, w in enumerate((2, 4, 8, 16)):
        main = np.where((ss <= tt) & (ss > tt - w), 1.0 / w, 0.0) - (ss == tt)
        halo = np.where(ss >= tt - w + 129, 1.0 / w, 0.0)
        first = np.where((ss <= tt) & (ss > tt - w), 1.0 / np.minimum(tt + 1, w), 0.0) - (ss == tt)
        for kind, m in enumerate((main, halo, first)):
            consts[:, C_B + (wi * 3 + kind) * 128:C_B + (wi * 3 + kind) * 128 + 128] = m.astype(f)

    pw = inp["a_pool_w"][0]
    wfull = np.zeros((768, 768), f)
    for g in range(4):
        wfull[192 * g:192 * g + 192, 192 * g:192 * g + 192] = pw[g]
    tiles_ = []
    for oc, srcs in {0: [0, 1], 1: [0, 1, 2], 2: [1, 2], 3: [3, 4], 4: [3, 4, 5], 5: [4, 5]}.items():
        for ch in srcs:
            tiles_.append(wfull[128 * ch:128 * ch + 128, 128 * oc:128 * oc + 128])
    wp = np.ascontiguousarray(np.stack(tiles_, axis=1)).reshape(128, 14 * 128)
    wfh = np.ascontiguousarray(kv_w[:, 1536:1548].reshape(KC, 128, NH).transpose(1, 0, 2)).reshape(128, KC * NH)
    return dict(wunits=units, cols=cols, fbc=fbc, consts=consts, wp=wp, wf=wfh)


def _to_fm(a):
    n, t, _ = a.shape
    return np.ascontiguousarray(a.reshape(n, t, KC, 128).transpose(0, 3, 2, 1))


def _from_fm(a):
    n, _, _, t = a.shape
    return np.ascontiguousarray(a.transpose(0, 3, 2, 1)).reshape(n, t, D)


_CACHE = {}


def run(inputs, n_cores, nseq, S, debug=False):
    inp = {k: np.asarray(v, dtype=np.float32) for k, v in inputs.items()}
    shared = _host_prepare(inp)
    key = (nseq, S, debug)
    if key not in _CACHE:
        _CACHE[key] = build_program(nseq, S, debug)[0]
    nc = _CACHE[key]
    in_maps = []
    for c in range(n_cores):
        m = dict(shared)
        m["xT"] = _to_fm(inp["x"][c * nseq:(c + 1) * nseq])
        m["memT"] = _to_fm(inp["mem"][c * nseq:(c + 1) * nseq])
        in_maps.append(m)
    res = run_bass_kernel_spmd(nc, in_maps, core_ids=list(range(n_cores)))
    out = np.concatenate([_from_fm(r["outT"]) for r in res.results], axis=0)
    if debug:
        dbg = np.concatenate([_from_fm(r["dbgT"]) for r in res.results], axis=0)
        return out, dbg
    return out


def kernel(**inputs):
    B, S, _ = inputs["x"].shape
    n_cores = 8
    return run(inputs, n_cores, B // n_cores, S).astype(np.float32)
```

```python
import numpy as np
import ml_dtypes
from contextlib import ExitStack
import concourse.bass as bass
import concourse.mybir as mybir
from concourse.bass_utils import run_bass_kernel_spmd

F32 = mybir.dt.float32
BF16 = mybir.dt.bfloat16
AF = mybir.ActivationFunctionType
ALU = mybir.AluOpType

D = 1024
KC = 8
T = 512
DFF = 2752
NPC = 22
NH = 12
MEM = 256
ALPHA = float((2.0 * 2) ** 0.25)
EPS = 1e-5
USZ = 4096
RING = 4
NEG = -30000.0

UNIT_NAMES = (["AIN0", "AIN1", "AOUT0", "AOUT1"] + ["UP0_%d" % i for i in range(11)]
              + ["DN0_%d_%d" % (hf, gi) for hf in range(2) for gi in range(3)]
              + ["KV0", "KV1", "KV2", "BQ0", "BQ1", "BQM", "BOUT0", "BOUT1"]
              + ["UP1_%d" % i for i in range(11)]
              + ["DN1_%d_%d" % (hf, gi) for hf in range(2) for gi in range(3)]
              + ["MKV0", "MKV1"])
UIDX = {n: i for i, n in enumerate(UNIT_NAMES)}
NU = len(UNIT_NAMES)


def unit_ncols(name):
    if name.startswith("BQ") and name != "BQM":
        return 390
    if name == "BQM":
        return 256
    return 512


def _cols_layout():
    m = {}
    n = 0
    for l in range(2):
        for nm in ("ln1g", "ln1b", "ln2g", "ln2b"):
            m[(nm, l)] = n
            n += 8
    m["pscale"] = n
    n += 6
    for l in range(2):
        for k in range(3):
            m[("cw", l, k)] = n
            n += 44
        m[("cb", l)] = n
        n += 44
    return m, n


COLMAP, NCOLS = _cols_layout()
C_U, C_ID, C_MASK, C_INVC, C_E = 0, 128, 256, 384, 448
C_B = 448 + 780
CW = C_B + 12 * 128


class Sched:
    def __init__(self):
        self.ops = []
        self.res = {}

    def add(self, eng, fn, reads=(), writes=(), dsem=None):
        idx = len(self.ops)
        deps = set()
        for r in reads:
            st = self.res.get(r)
            if st is not None and st[0] is not None:
                deps.add(st[0])
            if st is not None and isinstance(r, tuple) and r[0] == "ps":
                for (e2, _d), ix in st[1].items():
                    if e2 != eng:
                        deps.add(ix)
        for w in writes:
            st = self.res.get(w)
            if st is not None:
                if st[0] is not None:
                    deps.add(st[0])
                deps.update(st[1].values())
        deps.discard(idx)
        self.ops.append(dict(eng=eng, fn=fn, deps=sorted(deps), dsem=dsem, needed=False, val=None))
        for r in reads:
            st = self.res.setdefault(r, [None, {}])
            st[1][(eng, dsem)] = idx
        for w in writes:
            self.res[w] = [idx, {}]
        return idx

    def emit(self, nc, stack):
        ops = self.ops
        for op in ops:
            for d in op["deps"]:
                ops[d]["needed"] = True
        esem = {}
        for e in ("pe", "act", "dve", "pool"):
            esem[e] = stack.enter_context(nc.semaphore("s_" + e))
        dsem = {}
        cnt = {e: 0 for e in esem}
        dcnt = {}
        for op in ops:
            if op["dsem"] is not None:
                if op["dsem"] not in dsem:
                    dsem[op["dsem"]] = stack.enter_context(nc.semaphore("d_" + str(op["dsem"])))
                    dcnt[op["dsem"]] = 0
                dcnt[op["dsem"]] += 1
                op["val"] = 16 * dcnt[op["dsem"]]
            elif op["needed"]:
                cnt[op["eng"]] += 1
                op["val"] = cnt[op["eng"]]
        self.stats = dict(cnt=cnt, nops=len(ops))
        by_eng = {e: [] for e in ("pe", "act", "dve", "pool", "sp")}
        for i, op in enumerate(ops):
            by_eng[op["eng"]].append(i)

        def run(ename, e):
            seen = {}
            for i in by_eng[ename]:
                op = ops[i]
                need = {}
                for d in op["deps"]:
                    dop = ops[d]
                    if dop["dsem"] is not None:
                        key = ("d", dop["dsem"])
                    else:
                        key = ("e", dop["eng"])
                        if dop["eng"] == "pe" and ename == "pe":
                            continue
                    if dop["val"] > need.get(key, 0):
                        need[key] = dop["val"]
                for key, v in need.items():
                    if seen.get(key, 0) >= v:
                        continue
                    seen[key] = v
                    s = dsem[key[1]] if key[0] == "d" else esem[key[1]]
                    e.wait_ge(s, v)
                ins = op["fn"](e)
                if op["dsem"] is not None:
                    ins.then_inc(dsem[op["dsem"]], 16)
                elif op["needed"]:
                    ins.then_inc(esem[ename], 1)
            if ename == "sp":
                for k, s in dsem.items():
                    e.wait_ge(s, 16 * dcnt[k])

        with nc.Block() as block:
            @block.sync
            def _(e):
                run("sp", e)

            @block.tensor
            def _(e):
                run("pe", e)

            @block.scalar
            def _(e):
                run("act", e)

            @block.vector
            def _(e):
                run("dve", e)

            @block.gpsimd
            def _(e):
                run("pool", e)


class Rot:
    def __init__(self, name, n):
        self.name, self.n, self.i = name, n, -1

    def next(self):
        self.i = (self.i + 1) % self.n
        return self.i, (self.name, self.i)


def build_program(NSEQ, S, debug=False):
    NT = S // T
    NKT = S // 128
    nc = bass.Bass("TRN2", target_bir_lowering=False)
    stack = ExitStack()
    sch = Sched()

    def dram(name, shape, dt, kind):
        return nc.dram_tensor(name, list(shape), dt, kind=kind)

    xT = dram("xT", [NSEQ, 128, KC, S], F32, "ExternalInput")
    memT = dram("memT", [NSEQ, 128, KC, MEM], F32, "ExternalInput")
    wunits = dram("wunits", [NU, 128, USZ], F32, "ExternalInput")
    colsd = dram("cols", [128, NCOLS], F32, "ExternalInput")
    fbcd = dram("fbc", [128, NH], F32, "ExternalInput")
    constd = dram("consts", [128, CW], F32, "ExternalInput")
    wpd = dram("wp", [128, 14 * 128], F32, "ExternalInput")
    wfd = dram("wf", [128, KC * NH], F32, "ExternalInput")
    outT = dram("outT", [NSEQ, 128, KC, S], F32, "ExternalOutput")
    dbgT = dram("dbgT", [NSEQ, 128, KC, S], F32, "ExternalOutput") if debug else None
    wb = dram("wb", [NU, 128, USZ], BF16, "Internal")
    kcd = dram("kcache", [NSEQ, NH * 64, S], BF16, "Internal")
    vcd = dram("vcache", [NSEQ, S, NH * 64], BF16, "Internal")

    def sb(name, shape, dt):
        return stack.enter_context(nc.sbuf_tensor(name, list(shape), dt))

    wring = sb("wring", [128, RING, USZ], BF16)
    xr = sb("xr", [128, NSEQ, KC, T], F32)
    xb = sb("xb", [128, NSEQ, KC, T], BF16)
    scr = sb("scr", [128, NPC, T], BF16)
    utok = sb("utok", [128, 4, 768], BF16)
    uprev = sb("uprev", [128, NSEQ, 768], BF16)
    pT = sb("pT", [128, 4, T], BF16)
    rden = sb("rden", [128, 2, T], F32)
    tb = sb("tb", [128, 4, T], BF16)
    st = sb("st", [128, 4, T], F32)
    tf = sb("tf", [128, 2, T], F32)
    acc = sb("acc", [128, 4, T], F32)
    sg = sb("sg", [128, 2, T], BF16)
    chal = sb("chal", [128, NSEQ, 2, 44, 2], F32)
    hb = sb("hb", [128, 44, 2], F32)
    kt2 = sb("kt2", [128, 2, T], BF16)
    vt = sb("vt", [128, 2, 768], BF16)
    kst = sb("kst", [65, 3, 1024], BF16)
    vst = sb("vst", [128, 3, 8, 128], BF16)
    nFres = sb("nFres", [128, NSEQ, NKT, NH], F32)
    zt = sb("zt", [128, 4, NH], F32)
    lt = sb("lt", [128, 4, NH], F32)
    lb = sb("lb", [128, 4, NH], BF16)
    facc = sb("facc", [128, NSEQ, NH], F32)
    faccb = sb("faccb", [128, NSEQ, NH], BF16)
    ffm = sb("ffm", [NH, T], BF16)
    memKT = sb("memKT", [128, NSEQ, 2, 2, MEM], BF16)
    memV = sb("memV", [128, NSEQ, 2, 2, 4, 128], BF16)
    qmz = sb("qmz", [128, 4, T], BF16)
    memTb = sb("memTb", [128, KC, MEM], BF16)
    cols = sb("colsb", [128, NCOLS], F32)
    fbc = sb("fbcb", [128, NH], F32)
    cf = sb("cf", [128, 192], F32)
    cb = sb("cbf", [128, CW], BF16)
    wp = sb("wp_b", [128, 14 * 128], BF16)
    wf = sb("wf_b", [128, KC * NH], BF16)
    ones_b = sb("ones_b", [128, 128], BF16)
    ones_f = sb("ones_f", [128, 128], F32)
    epsc = sb("epsc", [128, 1], F32)
    onec = sb("onec", [128, 1], F32)
    ps = stack.enter_context(nc.psum_tensor("ps", [128, 8, T], F32))

    held = set()
    bank_ptr = [0]

    def nb():
        for _ in range(16):
            b = bank_ptr[0]
            bank_ptr[0] = (b + 1) % 8
            if b not in held:
                return b
        raise RuntimeError("no psum bank")

    def PS(b):
        return ("ps", b)

    def col(idx):
        return cols[:, idx:idx + 1]

    def pe_group(mms, reads, writes):
        def fn(e, mms=mms):
            ins = None
            for (o, l, r, s0, s1) in mms:
                ins = e.matmul(o, l, r, start=s0, stop=s1)
            return ins
        return sch.add("pe", fn, reads, writes)

    def act(out, in_, func, reads, writes, bias=None, scale=None):
        kw = {}
        if bias is not None:
            kw["bias"] = bias
        if scale is not None:
            kw["scale"] = scale
        return sch.add("act", lambda e: e.activation(out, in_, func, **kw), reads, writes)

    def dve(f, reads, writes):
        return sch.add("dve", f, reads, writes)

    def pool(f, reads, writes):
        return sch.add("pool", f, reads, writes)

    def dma(eng, out, in_, reads, writes, dsem):
        return sch.add(eng, lambda e: e.dma_start(out=out, in_=in_), reads, writes, dsem=dsem)

    order = []

    class Ring:
        def __init__(self):
            self.plan = []
            self.seen = set()
            self.issued = 0
            self.consumed = 0

        def set_plan(self, plan):
            self.plan = plan

        def _issue(self):
            k = self.issued
            name = self.plan[k]
            u = UIDX[name]
            n = KC * unit_ncols(name)
            slot = k % RING
            if name not in self.seen:
                self.seen.add(name)
                dma("pool", wring[:, slot, 0:n], wunits[u, :, 0:n], [], [("w", slot)], ("w", slot))
                dma("sp", wb[u, :, 0:n], wring[:, slot, 0:n], [("w", slot)], [("wb", u)], ("wbst", slot))
            else:
                dma("sp", wring[:, slot, 0:n], wb[u, :, 0:n], [("wb", u)], [("w", slot)], ("w", slot))
            self.issued += 1

        def get(self, name):
            k = self.consumed
            assert self.plan[k] == name, (self.plan[k], name)
            while self.issued < min(len(self.plan), k + RING - 1):
                self._issue()
            self.consumed += 1
            slot = k % RING
            return slot

    ring = Ring()
    plan = []
    for z in range(NSEQ):
        plan += ["MKV0", "MKV1"]
    M0U = ["AIN0", "AIN1", "AOUT0", "AOUT1"]
    F0U = ["UP0_%d" % i for i in range(11)] + ["DN0_%d_%d" % (hf, gi) for hf in range(2) for gi in range(3)]
    M1U = ["KV0", "KV1", "KV2", "BQ0", "BQ1", "BQM", "BOUT0", "BOUT1"]
    F1U = ["UP1_%d" % i for i in range(11)] + ["DN1_%d_%d" % (hf, gi) for hf in range(2) for gi in range(3)]
    for i in range(NT):
        for grp in (M0U, F0U, M1U, F1U):
            for z in range(NSEQ):
                plan += grp
    ring.set_plan(plan)

    def W(slot, ncols, kc, c0, c1, p0=0, p1=128):
        return wring[p0:p1, slot, kc * ncols + c0: kc * ncols + c1]

    dma("sp", cols[:, :], colsd[:, :], [], ["cols"], "i0")
    dma("sp", fbc[:, :], fbcd[:, :], [], ["fbc"], "i1")
    dma("sp", cf[:, 0:128], constd[:, 0:128], [], ["cf"], "i2")
    dma("sp", cf[:, 128:192], constd[:, C_INVC:C_INVC + 64], ["cf"], ["cf"], "i2b")
    dma("pool", cb[:, :], constd[:, :], [], ["cb"], "i3")
    dma("pool", wp[:, :], wpd[:, :], [], ["wp"], "i4")
    dma("pool", wf[:, :], wfd[:, :], [], ["wf"], "i5")
    pool(lambda e: e.memset(ones_b[:, :], 1.0), [], ["ones_b"])
    pool(lambda e: e.memset(ones_f[:, :], 1.0), [], ["ones_f"])
    pool(lambda e: e.memset(epsc[:, :], EPS), [], ["epsc"])
    pool(lambda e: e.memset(onec[:, :], 1.0), [], ["onec"])
    pool(lambda e: e.memset(kst[:, :, :], 1.0), [], [("kst", 0), ("kst", 1), ("kst", 2)])
    pool(lambda e: e.memset(vst[:, :, :, :], 1.0), [], [("vst", 0), ("vst", 1), ("vst", 2)])
    for z_ in range(NSEQ):
        for l_ in range(2):
            for m_ in range(2):
                pool(lambda e, z_=z_, l_=l_, m_=m_: e.memset(memV[:, z_, l_, m_, :, :], 1.0), [("memV", z_, l_)], [("memV", z_, l_)])
    pool(lambda e: e.memset(qmz[:, :, :], 0.0), [], [("qmz", h) for h in range(4)])

    def H(j, p1=128):
        return scr[0:p1, j, :], ("S", j)

    def MIX(c, p0=0, p1=128):
        return scr[p0:p1, c, :], ("S", c)

    def POOLED(c, p0=0, p1=128):
        return scr[p0:p1, 8 + c, :], ("S", 8 + c)

    def QAUG(h, c0=0, c1=T):
        return scr[0:65, 8 + h, c0:c1], ("S", 8 + h)

    def QM(j, p0=0, p1=128):
        return scr[p0:p1, 20 + j, :], ("S", 20 + j)

    cur = [0]

    def XRN(c, p=None):
        return ("xr", cur[0] if p is None else p, c)

    def XRV(c, p=None):
        return xr[:, cur[0] if p is None else p, c, :]

    def XBN(c):
        return ("xb", cur[0], c)

    def XBL():
        return [("xb", cur[0], c) for c in range(KC)]


    tbrot = Rot("tb", 4)
    tfrot = Rot("tf", 2)
    ln_state = {}

    late = []

    def drain_late(n=None, stream=None):
        saved = cur[0]
        if stream is not None:
            keep = []
            for (z, f) in list(late):
                if z == stream:
                    cur[0] = z
                    f()
                else:
                    keep.append((z, f))
            late[:] = keep
        else:
            k = len(late) if n is None else min(n, len(late))
            for _ in range(k):
                z, f = late.pop(0)
                cur[0] = z
                f()
        cur[0] = saved

    def ln_begin(drain=True):
        if drain:
            drain_late()
        ln_state["b"] = None
        ln_state["n"] = 0

    def ln_banks():
        if ln_state["b"] is None:
            b1 = nb()
            held.add(b1)
            b2 = nb()
            held.add(b2)
            ln_state["b"] = (b1, b2)
        return ln_state["b"]

    def res_chunk(oc, bank):
        xv, xn = XRV(oc), XRN(oc)
        dve(lambda e: e.scalar_tensor_tensor(out=xv, in0=xv, scalar=ALPHA,
                                             in1=ps[:, bank, :], op0=ALU.mult, op1=ALU.add),
            [PS(bank), xn], [xn])
        i1, n1 = tbrot.next()
        act(tb[:, i1, :], xv, AF.Copy, [xn], [n1])
        i2, n2 = tbrot.next()
        act(tb[:, i2, :], xv, AF.Square, [xn], [n2])
        ln_state.setdefault("pend", []).append((i1, n1, i2, n2))

    def stat_flush(keep=0):
        pend = ln_state.setdefault("pend", [])
        if len(pend) <= keep:
            return
        b1, b2 = ln_banks()
        while len(pend) > keep:
            i1, n1, i2, n2 = pend.pop(0)
            k = ln_state["n"]
            ln_state["n"] += 1
            pe_group([(ps[:, b1, :], ones_b[:, :], tb[:, i1, :], k == 0, k == KC - 1),
                      (ps[:, b2, :], ones_b[:, :], tb[:, i2, :], k == 0, k == KC - 1)],
                     [n1, n2, "ones_b"], [PS(b1), PS(b2)])

    def ln_finish(gkey, bkey, l, write_xb=True):
        stat_flush(0)
        b1, b2 = ln_banks()
        mean, var, rstd, nmr = st[:, 0, :], st[:, 1, :], st[:, 2, :], st[:, 3, :]
        act(mean, ps[:, b1, :], AF.Copy, [PS(b1)], [("st", 0)], scale=1.0 / D)
        dve(lambda e: e.tensor_tensor(out=var, in0=mean, in1=mean, op=ALU.mult), [("st", 0)], [("st", 1)])
        dve(lambda e: e.scalar_tensor_tensor(out=var, in0=ps[:, b2, :], scalar=1.0 / D, in1=var,
                                             op0=ALU.mult, op1=ALU.subtract),
            [PS(b2), ("st", 1)], [("st", 1)])
        held.discard(b1)
        held.discard(b2)
        z0 = cur[0]

        def tail_head():
            act(rstd, var, AF.Ln, [("st", 1), "epsc"], [("st", 2)], bias=epsc[:, 0:1], scale=1.0)
            act(rstd, rstd, AF.Exp, [("st", 2)], [("st", 2)], scale=-0.5)
            dve(lambda e: e.scalar_tensor_tensor(out=nmr, in0=mean, scalar=-1.0, in1=rstd,
                                                 op0=ALU.mult, op1=ALU.mult),
                [("st", 0), ("st", 2)], [("st", 3)])
        late.append((z0, tail_head))
        g0 = COLMAP[(gkey, l)]
        b0 = COLMAP[(bkey, l)]

        def chunk(c):
            i, n = tfrot.next()
            xv, xn = XRV(c), XRN(c)
            dve(lambda e, i=i, xv=xv: e.tensor_tensor(out=tf[:, i, :], in0=xv, in1=rstd, op=ALU.mult),
                [xn, ("st", 2)], [n])
            dve(lambda e, i=i: e.tensor_tensor(out=tf[:, i, :], in0=tf[:, i, :], in1=nmr, op=ALU.add),
                [n, ("st", 3)], [n])
            if write_xb:
                act(xb[:, cur[0], c, :], tf[:, i, :], AF.Identity, [n, "cols"], [XBN(c)],
                    bias=col(b0 + c), scale=col(g0 + c))
            act(xv, tf[:, i, :], AF.Identity, [n, "cols"], [xn],
                bias=col(b0 + c), scale=col(g0 + c))
        for c in range(KC):
            late.append((z0, (lambda c=c: chunk(c))))

    ptrot = Rot("pT", 4)
    rdrot = Rot("rden", 2)

    def qm_evac(j, bank):
        act(qmz[0:64, 2 * j, :], ps[0:64, bank, :], AF.Copy, [PS(bank)], [("qmz", 2 * j)])
        act(qmz[64:128, 2 * j + 1, :], ps[64:128, bank, :], AF.Copy, [PS(bank)], [("qmz", 2 * j + 1)])

    def mem_attention(l):
        steps = [(h, mc) for h in range(4) for mc in range(2)]
        stt = {}

        def qk(h, mc):
            if mc == 0:
                bh = nb()
                held.add(bh)
                stt[("b", h)] = bh
            bl = nb()
            pe_group([(ps[:, bl, :], memKT[:, cur[0], l, h // 2, mc * 128:(mc + 1) * 128], qmz[:, h, :], True, True)],
                     [("qmz", h), ("memKT", cur[0], l)], [PS(bl)])
            pi, pn = ptrot.next()
            act(pT[:, pi, :], ps[:, bl, :], AF.Exp, [PS(bl)], [pn], scale=0.125)
            stt[(h, mc)] = (pi, pn)

        def pv(h, mc):
            pi, pn = stt.pop((h, mc))
            bh = stt[("b", h)]
            pe_group([(ps[:, bh, :], memV[:, cur[0], l, mc, h, :], pT[:, pi, :], mc == 0, mc == 1)],
                     [pn, ("memV", cur[0], l)], [PS(bh)])
            if mc == 1:
                hp = 64 * (h % 2)
                ri, rn = rdrot.next()
                act(rden[0:64, ri, :], ps[64:128, bh, :], AF.Ln, [PS(bh)], [rn])
                act(rden[0:64, ri, :], rden[0:64, ri, :], AF.Exp, [rn], [rn], scale=-1.0)
                mv, mn = MIX(6 + h // 2, hp, hp + 64)
                dve(lambda e, ri=ri, bh=bh, mv=mv: e.tensor_tensor(out=mv, in0=ps[0:64, bh, :], in1=rden[0:64, ri, :], op=ALU.mult),
                    [PS(bh), rn, mn], [mn])
                held.discard(bh)

        SK = 2
        for n in range(len(steps) + SK):
            if n < len(steps):
                qk(*steps[n])
            if n - SK >= 0:
                pv(*steps[n - SK])

    pre_ln = []

    def run_pre_ln():
        while pre_ln:
            pre_ln.pop(0)()

    def wout_ln(unames, l):
        ln_begin()
        for oc in range(KC):
            if oc % 4 == 0:
                slot = ring.get(unames[oc // 4])
            b = nb()
            mms = []
            for kc in range(KC):
                mv, mn = MIX(kc)
                mms.append((ps[:, b, :], W(slot, 512, kc, (oc % 4) * 128, (oc % 4) * 128 + 128), mv, kc == 0, kc == KC - 1))
            pe_group(mms, [("w", slot)] + [("S", c) for c in range(KC)], [PS(b)])
            res_chunk(oc, b)
            stat_flush(1)
        run_pre_ln()
        ln_finish("ln1g", "ln1b", l)

    accrot = Rot("acc", 4)
    sgrot = Rot("sg", 2)

    def ffn(l, after_up=None, after_lnbegin=None):
        cw = [COLMAP[("cw", l, k)] for k in range(3)]
        cbi = COLMAP[("cb", l)]
        halr = [("chal", cur[0], l, ci) for ci in range(44)]
        h0, h1 = chal[:, cur[0], l, :, 0], chal[:, cur[0], l, :, 1]
        pool(lambda e: e.tensor_tensor(out=hb[:, :, 0], in0=h1, in1=cols[:, cw[1]:cw[1] + 44], op=ALU.mult), halr + ["cols"], ["hb"])
        pool(lambda e: e.tensor_tensor(out=hb[:, :, 1], in0=h0, in1=cols[:, cw[0]:cw[0] + 44], op=ALU.mult), halr + ["cols", "hb"], ["hb"])
        pool(lambda e: e.tensor_tensor(out=hb[:, :, 0], in0=hb[:, :, 0], in1=hb[:, :, 1], op=ALU.add), ["hb"], ["hb"])
        pool(lambda e: e.tensor_tensor(out=hb[:, :, 1], in0=h1, in1=cols[:, cw[0]:cw[0] + 44], op=ALU.mult), halr + ["cols", "hb"], ["hb"])
        gate_pending = []
        for i in range(11):
            slot = ring.get("UP%d_%d" % (l, i))
            for jj in range(2):
                j = 2 * i + jj
                M = 128
                br = []
                for ug in range(2):
                    ci = ug * 22 + j
                    b = nb()
                    mms = []
                    for kc in range(KC):
                        mms.append((ps[0:M, b, :], wring[:, slot, kc * 512 + ug * 256 + jj * 128: kc * 512 + ug * 256 + jj * 128 + M],
                                    xb[:, cur[0], kc, :], kc == 0, kc == KC - 1))
                    pe_group(mms, [("w", slot)] + XBL(), [PS(b)])
                    ai, an = accrot.next()
                    br.append((ci, b, ai, an))
                for (ci, b, ai, an) in br:
                    act(chal[0:M, cur[0], l, ci, 0:2], ps[0:M, b, T - 2:T], AF.Copy, [PS(b)], [("chal", cur[0], l, ci)])
                    act(acc[0:M, ai, :], ps[0:M, b, :], AF.Identity, [PS(b), "cols"], [an],
                        bias=cols[0:M, cbi + ci:cbi + ci + 1], scale=cols[0:M, cw[2] + ci:cw[2] + ci + 1])
                for (ci, b, ai, an) in br:
                    w1 = cols[0:M, cw[1] + ci:cw[1] + ci + 1]
                    dve(lambda e, ai=ai, b=b, M=M, w1=w1: e.scalar_tensor_tensor(
                        out=acc[0:M, ai, 1:T], in0=ps[0:M, b, 0:T - 1], scalar=w1, in1=acc[0:M, ai, 1:T],
                        op0=ALU.mult, op1=ALU.add), [PS(b), an, "cols"], [an])
                for (ci, b, ai, an) in br:
                    w0 = cols[0:M, cw[0] + ci:cw[0] + ci + 1]
                    dve(lambda e, ai=ai, b=b, M=M, w0=w0: e.scalar_tensor_tensor(
                        out=acc[0:M, ai, 2:T], in0=ps[0:M, b, 0:T - 2], scalar=w0, in1=acc[0:M, ai, 2:T],
                        op0=ALU.mult, op1=ALU.add), [PS(b), an, "cols"], [an])
                for (ci, b, ai, an) in br:
                    dve(lambda e, ai=ai, M=M, ci=ci: e.tensor_tensor(out=acc[0:M, ai, 0:2], in0=acc[0:M, ai, 0:2],
                                                                     in1=hb[0:M, ci, :], op=ALU.add), ["hb", an], [an])
                if gate_pending:
                    gate_pending.pop(0)()

                def gate(br=br, j=j, M=M):
                    (_, _, au, aun), (_, _, ag, agn) = br
                    si, sn = sgrot.next()
                    act(sg[0:M, si, :], acc[0:M, ag, :], AF.Silu, [agn], [sn])
                    hv, hn2 = H(j, M)
                    pool(lambda e, hv=hv, si=si, au=au, M=M: e.tensor_tensor(out=hv, in0=sg[0:M, si, :], in1=acc[0:M, au, :], op=ALU.mult),
                         [sn, aun], [hn2])
                gate_pending.append(gate)
        while gate_pending:
            gate_pending.pop(0)()
        if after_up is not None:
            after_up()
        ln_begin(drain=(after_lnbegin is not None))
        if after_lnbegin is not None:
            after_lnbegin()
        for hf in range(2):
            banks = []
            for o in range(4):
                b = nb()
                held.add(b)
                banks.append(b)
            for gi in range(3):
                slot = ring.get("DN%d_%d_%d" % (l, hf, gi))
                mms = []
                reads = [("w", slot)]
                for jl in range(8):
                    j = 8 * gi + jl
                    if j >= NPC:
                        break
                    K = 128
                    reads.append(("S", j))
                    for o in range(4):
                        mms.append((ps[:, banks[o], :], wring[0:K, slot, jl * 512 + o * 128: jl * 512 + o * 128 + 128],
                                    scr[0:K, j, :], j == 0, j == NPC - 1))
                pe_group(mms, reads, [PS(b) for b in banks])
            for o in range(4):
                held.discard(banks[o])
                res_chunk(4 * hf + o, banks[o])
                stat_flush(1)
            if hf == 0:
                drain_late()
        run_pre_ln()
        ln_finish("ln2g", "ln2b", l, write_xb=(l == 0))


    def seq_setup(s):
        pool(lambda e: e.memset(chal[:, s, :, :, :], 0.0), [], [("chal", s, l, ci) for l in range(2) for ci in range(44)])
        pool(lambda e: e.memset(facc[:, s, :], 0.0), [], [("facc", s)])
        pool(lambda e: e.memset(faccb[:, s, :], 0.0), [], [("faccb", s)])
        dma("pool", memTb[:, :, :], memT[s, :, :, :], [], [("memTb", kc) for kc in range(KC)], "memTb")
        for l in range(2):
            slot = ring.get("MKV%d" % l)
            for j in range(2):
                b = nb()
                mms = [(ps[:, b, 0:MEM], W(slot, 512, kc, j * 128, j * 128 + 128), memTb[:, kc, :], kc == 0, kc == KC - 1)
                       for kc in range(KC)]
                pe_group(mms, [("w", slot)] + [("memTb", kc) for kc in range(KC)], [PS(b)])
                act(memKT[:, s, l, j, :], ps[:, b, 0:MEM], AF.Copy, [PS(b)], [("memKT", s, l)])
            for mc in range(2):
                b = nb()
                mms = [(ps[:, b, 0:MEM], memTb[:, kc, mc * 128:(mc + 1) * 128], W(slot, 512, kc, 256, 512), kc == 0, kc == KC - 1)
                       for kc in range(KC)]
                pe_group(mms, [("w", slot)] + [("memTb", kc) for kc in range(KC)], [PS(b)])
                for h in range(4):
                    act(memV[:, s, l, mc, h, 0:64], ps[:, b, 64 * h:64 * h + 64], AF.Copy, [PS(b), ("memV", s, l)], [("memV", s, l)])

    WIN = {0: [(0, 128, 2)], 1: [(0, 64, 2), (64, 128, 4)], 2: [(0, 128, 4)], 3: [(0, 128, 8)],
           4: [(0, 64, 8), (64, 128, 16)], 5: [(0, 128, 16)]}

    def layer0_mixer(first):
        z = cur[0]
        s_in = [ring.get("AIN0"), ring.get("AIN1")]
        for sub in range(4):
            b1 = nb()
            b2 = nb()
            tk = xb[:, z, :, sub * 128:(sub + 1) * 128]
            mms = [(ps[:, b1, :], xb[:, z, kc, sub * 128:(sub + 1) * 128], W(s_in[0], 512, kc, 0, 512), kc == 0, kc == KC - 1)
                   for kc in range(KC)]
            mms += [(ps[:, b2, 0:256], xb[:, z, kc, sub * 128:(sub + 1) * 128], W(s_in[1], 512, kc, 0, 256), kc == 0, kc == KC - 1)
                    for kc in range(KC)]
            pe_group(mms, [("w", s_in[0]), ("w", s_in[1])] + XBL(), [PS(b1), PS(b2)])
            drain_late(1)
            act(utok[:, sub, 0:512], ps[:, b1, :], AF.Copy, [PS(b1)], [("utok", sub)])
            act(utok[:, sub, 512:768], ps[:, b2, 0:256], AF.Copy, [PS(b2), ("utok", sub)], [("utok", sub)])
        for oc in (6, 7):
            b = nb()
            mms = [(ps[:, b, :], W(s_in[1], 512, kc, (oc % 4) * 128, (oc % 4) * 128 + 128), xb[:, z, kc, :], kc == 0, kc == KC - 1)
                   for kc in range(KC)]
            pe_group(mms, [("w", s_in[1])] + XBL(), [PS(b)])
            drain_late(1)
            qm_evac(oc - 6, b)
        for c in range(6):
            for (p0, p1, w) in WIN[c]:
                wi = {2: 0, 4: 1, 8: 2, 16: 3}[w]
                bmain = cb[:, C_B + (wi * 3 + 0) * 128:C_B + (wi * 3 + 0) * 128 + 128]
                bhalo = cb[:, C_B + (wi * 3 + 1) * 128:C_B + (wi * 3 + 1) * 128 + 128]
                bfirst = cb[:, C_B + (wi * 3 + 2) * 128:C_B + (wi * 3 + 2) * 128 + 128]
                bv = nb()
                mms = []
                reads = ["cb"] + [("utok", sub) for sub in range(4)]
                for sub in range(4):
                    o = ps[:, bv, sub * 128:(sub + 1) * 128]
                    cur_u = utok[:, sub, c * 128:(c + 1) * 128]
                    if sub == 0 and first:
                        mms.append((o, cur_u, bfirst, True, True))
                        continue
                    mms.append((o, cur_u, bmain, True, False))
                    if sub == 0:
                        mms.append((o, uprev[:, z, c * 128:(c + 1) * 128], bhalo, False, True))
                        reads.append(("uprev", z))
                    else:
                        mms.append((o, utok[:, sub - 1, c * 128:(c + 1) * 128], bhalo, False, True))
                pe_group(mms, reads, [PS(bv)])
                pv, pn = POOLED(c, p0, p1)
                act(pv, ps[p0:p1, bv, :], AF.Copy, [PS(bv), pn], [pn])
        dve(lambda e, z=z: e.tensor_copy(uprev[:, z, :], utok[:, 3, :]), [("utok", 3)], [("uprev", z)])

    GL_TILES = {0: [0, 1], 1: [0, 1, 2], 2: [1, 2], 3: [3, 4], 4: [3, 4, 5], 5: [4, 5]}

    def layer0_glinear():
        PSC = COLMAP["pscale"]
        ti = 0
        for oc in range(6):
            b = nb()
            mms = []
            reads = ["wp"]
            srcs = GL_TILES[oc]
            for k, ch in enumerate(srcs):
                pv, pn = POOLED(ch)
                reads.append(pn)
                mms.append((ps[:, b, :], wp[:, ti * 128:(ti + 1) * 128], pv, k == 0, k == len(srcs) - 1))
                ti += 1
            pe_group(mms, reads, [PS(b)])
            mv, mn = MIX(oc)
            act(mv, ps[:, b, :], AF.Identity, [PS(b), "cols"], [mn], scale=col(PSC + oc))

    ktrot = Rot("kt2", 2)
    vtrot = Rot("vt", 2)
    ztrot = Rot("zt", 4)

    def layer1_proj(s, i):
        t0 = i * T
        bfm = nb()
        held.add(bfm)
        fst = {}

        def stageA(sub):
            b = nb()
            mms = [(ps[:, b, 0:NH], xb[:, cur[0], kc, sub * 128:(sub + 1) * 128], wf[:, kc * NH:(kc + 1) * NH], kc == 0, kc == KC - 1)
                   for kc in range(KC)]
            pe_group(mms, ["wf"] + XBL(), [PS(b)])
            zi, zn = ztrot.next()
            ltn, lbn = ("lt", zi), ("lb", zi)
            dve(lambda e, zi=zi, b=b: e.tensor_tensor(out=zt[:, zi, :], in0=ps[:, b, 0:NH], in1=fbc[:, :], op=ALU.add),
                [PS(b), "fbc"], [zn])
            act(zt[:, zi, :], zt[:, zi, :], AF.Exp, [zn], [zn], scale=-1.0)
            act(lt[:, zi, :], zt[:, zi, :], AF.Ln, [zn, "onec"], [ltn], bias=onec[:, 0:1], scale=1.0)
            dve(lambda e, zi=zi: e.tensor_copy(lb[:, zi, :], lt[:, zi, :]), [ltn], [lbn])
            fst[sub] = (zi, ltn, lbn)

        def stageB(sub):
            zi, ltn, lbn = fst[sub]
            gsub = 4 * i + sub
            b2 = nb()
            pe_group([(ps[:, b2, 0:NH], cf[:, C_U:C_U + 128], lt[:, zi, :], True, False),
                      (ps[:, b2, 0:NH], ones_f[:, :], facc[:, cur[0], :], False, True)],
                     ["cf", ltn, "ones_f", ("facc", cur[0])], [PS(b2)])
            pe_group([(ps[0:NH, bfm, sub * 128:(sub + 1) * 128], lb[:, zi, :], cb[:, C_U:C_U + 128], True, False),
                      (ps[0:NH, bfm, sub * 128:(sub + 1) * 128], faccb[:, cur[0], :], ones_b[:, :], False, True)],
                     [lbn, "cb", ("faccb", cur[0]), "ones_b"], [PS(bfm)])
            dve(lambda e, b2=b2, gsub=gsub, z=cur[0]: e.tensor_copy(nFres[:, z, gsub, :], ps[:, b2, 0:NH]), [PS(b2)], [("nF", cur[0], gsub)])
            dve(lambda e, zi=zi, z=cur[0]: e.tensor_tensor(out=facc[:, z, :], in0=facc[:, z, :], in1=lt[:, zi, :], op=ALU.add),
                [ltn, ("facc", cur[0])], [("facc", cur[0])])
            dve(lambda e, z=cur[0]: e.tensor_copy(faccb[:, z, :], facc[:, z, :]), [("facc", cur[0])], [("faccb", cur[0])])

        s0 = ring.get("KV0")
        s1 = ring.get("KV1")

        def kgroup(j):
            slot, cofs = (s0, 128 * j) if j < 4 else (s1, 128 * (j - 4))
            b = nb()
            mms = [(ps[:, b, :], W(slot, 512, kc, cofs, cofs + 128), xb[:, cur[0], kc, :], kc == 0, kc == KC - 1) for kc in range(KC)]
            pe_group(mms, [("w", slot)] + XBL(), [PS(b)])
            drain_late(1)
            r, rn = ktrot.next()
            act(kt2[:, r, :], ps[:, b, :], AF.Copy, [PS(b)], [rn])
            dma("sp", kcd[s, 128 * j:128 * j + 128, t0:t0 + T], kt2[:, r, :], [rn], [("kc", s, j, i)], rn)

        def vgroup(sub, s2):
            b1 = nb()
            b2 = nb()
            mms = [(ps[:, b1, 0:256], xb[:, cur[0], kc, sub * 128:(sub + 1) * 128], W(s1, 512, kc, 256, 512), kc == 0, kc == KC - 1)
                   for kc in range(KC)]
            mms += [(ps[:, b2, :], xb[:, cur[0], kc, sub * 128:(sub + 1) * 128], W(s2, 512, kc, 0, 512), kc == 0, kc == KC - 1)
                    for kc in range(KC)]
            pe_group(mms, [("w", s1), ("w", s2)] + XBL(), [PS(b1), PS(b2)])
            drain_late(1)
            r, rn = vtrot.next()
            act(vt[:, r, 0:256], ps[:, b1, 0:256], AF.Copy, [PS(b1)], [rn])
            act(vt[:, r, 256:768], ps[:, b2, :], AF.Copy, [PS(b2), rn], [rn])
            dma("sp", vcd[s, t0 + sub * 128:t0 + sub * 128 + 128, :], vt[:, r, :], [rn], [("vc", s, i, sub)], rn)

        for sub in range(4):
            stageA(sub)
        kgroup(0)
        kgroup(1)
        stageB(0)
        kgroup(2)
        kgroup(3)
        stageB(1)
        kgroup(4)
        kgroup(5)
        stageB(2)
        s2 = ring.get("KV2")
        vgroup(0, s2)
        vgroup(1, s2)
        stageB(3)
        act(ffm[:, :], ps[0:NH, bfm, :], AF.Copy, [PS(bfm)], ["ffm"], scale=-1.0)
        held.discard(bfm)
        vgroup(2, s2)
        vgroup(3, s2)
        for h in range(NH):
            if h % 6 == 0:
                slot = ring.get("BQ%d" % (h // 6))
            b = nb()
            hc = (h % 6) * 65
            mms = [(ps[0:65, b, :], W(slot, 390, kc, hc, hc + 65), xb[:, cur[0], kc, :], kc == 0, False) for kc in range(KC)]
            mms.append((ps[0:65, b, :], cb[0:NH, C_E + 65 * h:C_E + 65 * h + 65], ffm[:, :], False, True))
            qv, qn = QAUG(h)
            pe_group(mms, [("w", slot), "cb", "ffm"] + XBL(), [PS(b)])
            act(qv, ps[0:65, b, :], AF.Copy, [PS(b)], [qn], scale=0.125)
        slot = ring.get("BQM")
        for j in range(2):
            b = nb()
            mms = [(ps[:, b, :], W(slot, 256, kc, j * 128, j * 128 + 128), xb[:, cur[0], kc, :], kc == 0, kc == KC - 1) for kc in range(KC)]
            pe_group(mms, [("w", slot)] + XBL(), [PS(b)])
            qm_evac(j, b)

    ksrot = Rot("kst", 3)
    vsrot = Rot("vst", 3)

    def fox_prefetch(s, i):
        nkt = 4 * (i + 1)
        n_here = min(8, nkt)
        nk = 128 * n_here
        ki, kn = ksrot.next()
        vi, vn = vsrot.next()
        kreads = [("kc", s, 0, ti) for ti in range(0, (nk - 1) // T + 1)]
        vreads = [("vc", s, a_ // 4, a_ % 4) for a_ in range(n_here)]
        dma("sp", kst[0:64, ki, 0:nk], kcd[s, 0:64, 0:nk], kreads, [kn], kn)
        vsrc = vcd.rearrange("s (kt p) d -> s p kt d", p=128)[s, :, 0:n_here, 0:64]
        dma("sp", vst[:, vi, 0:n_here, 0:64], vsrc, vreads, [vn], vn)
        return {(0, 0): (ki, kn, vi, vn)}

    def fox_attention(s, i, pre=None):
        nkt = 4 * (i + 1)
        for b in (6, 7):
            held.add(b)
        steps = []
        for h in range(NH):
            hp = 64 * (h % 2)
            ba = 6 + (h % 2)
            nch = (nkt + 7) // 8
            for ck in range(nch):
                kt0 = 8 * ck
                n_here = min(8, nkt - kt0)
                for kk in range(n_here):
                    steps.append((h, hp, ba, ck, kt0, n_here, kk))
        state = dict(pre or {})

        def qk(st_):
            h, hp, ba, ck, kt0, n_here, kk = st_
            if kk == 0 and (h, ck) not in state:
                nk = 128 * n_here
                k0 = 128 * kt0
                ki, kn = ksrot.next()
                vi, vn = vsrot.next()
                kreads = [("kc", s, h // 2, ti) for ti in range(k0 // T, (k0 + nk - 1) // T + 1)]
                vreads = [("vc", s, (kt0 + a_) // 4, (kt0 + a_) % 4) for a_ in range(n_here)]
                dma("sp", kst[0:64, ki, 0:nk], kcd[s, 64 * h:64 * h + 64, k0:k0 + nk], kreads, [kn], kn)
                vsrc = vcd.rearrange("s (kt p) d -> s p kt d", p=128)[s, :, kt0:kt0 + n_here, 64 * h:64 * h + 64]
                dma("sp", vst[:, vi, 0:n_here, 0:64], vsrc, vreads, [vn], vn)
                state[(h, ck)] = (ki, kn, vi, vn)
            ki, kn, vi, vn = state[(h, ck)]
            kt = kt0 + kk
            jj = kt - 4 * i
            q0 = 128 * jj if jj >= 0 else 0
            N = T - q0
            bl = nb()
            qv, qn = QAUG(h, q0, T)
            mms = [(ps[:, bl, 0:N], kst[0:65, ki, kk * 128:(kk + 1) * 128], qv, True, jj < 0)]
            reads = [kn, qn]
            if jj >= 0:
                mms.append((ps[:, bl, 0:128], cb[:, C_ID:C_ID + 128], cb[:, C_MASK:C_MASK + 128], False, True))
                reads.append("cb")
            pe_group(mms, reads, [PS(bl)])
            pi, pn = ptrot.next()
            act(pT[:, pi, 0:N], ps[:, bl, 0:N], AF.Exp, [PS(bl), ("nF", cur[0], kt)], [pn],
                bias=nFres[:, cur[0], kt, h:h + 1], scale=1.0)
            state[st_] = (pi, pn, q0, N, kt, vi, vn)

        def pv(st_):
            h, hp, ba, ck, kt0, n_here, kk = st_
            pi, pn, q0, N, kt, vi, vn = state.pop(st_)
            pe_group([(ps[:, ba, q0:T], vst[:, vi, kk, :], pT[:, pi, 0:N], kt == 0, kt == nkt - 1)],
                     [vn, pn], [PS(ba)])
            if kt == nkt - 1:
                ri, rn = rdrot.next()
                if h == NH - 1:
                    act(rden[0:64, ri, :], ps[64:128, ba, :], AF.Ln, [PS(ba)], [rn])
                    act(rden[0:64, ri, :], rden[0:64, ri, :], AF.Exp, [rn], [rn], scale=-1.0)
                else:
                    dve(lambda e, ri=ri, ba=ba: e.reciprocal(rden[0:64, ri, :], ps[64:128, ba, :]), [PS(ba)], [rn])
                mv, mn = MIX(h // 2, hp, hp + 64)
                dve(lambda e, ri=ri, ba=ba, mv=mv: e.tensor_tensor(out=mv, in0=ps[0:64, ba, :],
                                                                  in1=rden[0:64, ri, :], op=ALU.mult),
                    [PS(ba), rn, mn], [mn])

        SK = 2
        for n in range(len(steps) + SK):
            if n < len(steps):
                qk(steps[n])
            if n - SK >= 0:
                pv(steps[n - SK])
            if n % 3 == 0:
                drain_late(1)
        for b in (6, 7):
            held.discard(b)

    def load_x(z, i):
        dma("pool", xr[:, z, :, :], xT[z, :, :, i * T:i * T + T], [], [("xr", z, c) for c in range(KC)], ("x", z))

    def cast_x(z):
        for c in range(0, KC, 2):
            dve(lambda e, c=c, z=z: e.tensor_copy(xb[:, z, c:c + 2, :], xr[:, z, c:c + 2, :]),
                [("xr", z, c), ("xr", z, c + 1)], [("xb", z, c), ("xb", z, c + 1)])

    need_cast = set()

    def finish_tile(z, i):
        drain_late(stream=z)
        dma("pool", outT[z, :, :, i * T:i * T + T], xr[:, z, :, :], [("xr", z, c) for c in range(KC)], [], ("out", z))
        if i + 1 < NT:
            load_x(z, i + 1)
            need_cast.add(z)

            def early_cast(z=z):
                if z in need_cast:
                    need_cast.discard(z)
                    cast_x(z)
            pre_ln.append(early_cast)

    for z in range(NSEQ):
        load_x(z, 0)
    for z in range(NSEQ):
        cur[0] = z
        seq_setup(z)
        cast_x(z)
    for i in range(NT):
        t0 = i * T
        for z in range(NSEQ):
            cur[0] = z
            drain_late(stream=z)
            if z in need_cast:
                need_cast.discard(z)
                cast_x(z)
            layer0_mixer(first=(i == 0))
            if NSEQ == 2 and z == 0 and i > 0:
                finish_tile(1, i - 1)
                cur[0] = z
            mem_attention(0)
            layer0_glinear()
            wout_ln(["AOUT0", "AOUT1"], 0)
        for z in range(NSEQ):
            cur[0] = z
            drain_late(stream=z)
            ffn(0)
            if debug:
                drain_late(stream=z)
            if debug:
                dma("sp", dbgT[z, :, :, t0:t0 + T], xr[:, z, :, :], [XRN(c) for c in range(KC)], [], ("dbg", z))
        for z in range(NSEQ):
            cur[0] = z
            drain_late(stream=z)
            layer1_proj(z, i)
            pre = fox_prefetch(z, i)
            mem_attention(1)
            fox_attention(z, i, pre)
            wout_ln(["BOUT0", "BOUT1"], 1)
        for z in range(NSEQ):
            cur[0] = z
            drain_late(stream=z)
            if NSEQ == 2 and z == 1:
                def hook(i=i):
                    finish_tile(0, i)
                    cur[0] = 1
                ffn(1, after_lnbegin=hook)
            else:
                ffn(1)
            if NSEQ == 1:
                finish_tile(z, i)
    if NSEQ == 2:
        finish_tile(1, NT - 1)

    drain_late()
    assert ring.consumed == len(plan), (ring.consumed, len(plan))
    sch.emit(nc, stack)
    stack.close()
    return nc, sch


def _unit(Wsub):
    n = Wsub.shape[1]
    a = np.ascontiguousarray(Wsub.reshape(KC, 128, n).transpose(1, 0, 2)).reshape(128, KC * n)
    out = np.zeros((128, USZ), np.float32)
    out[:, :KC * n] = a
    return out


def _host_prepare(inp):
    f = np.float32
    units = np.zeros((NU, 128, USZ), f)
    a_w_in, a_w_out = inp["a_w_in"][0], inp["a_w_out"][0]
    b_w_q, b_w_out = inp["b_w_q"][0], inp["b_w_out"][0]
    kv_w = inp["kv_w"]
    for hfi in range(2):
        units[UIDX["AIN%d" % hfi]] = _unit(a_w_in[:, 512 * hfi:512 * hfi + 512])
        units[UIDX["AOUT%d" % hfi]] = _unit(a_w_out[:, 512 * hfi:512 * hfi + 512])
        units[UIDX["BOUT%d" % hfi]] = _unit(b_w_out[:, 512 * hfi:512 * hfi + 512])
    for l in range(2):
        Wup = inp["ffn_w_up"][l]
        Wd = inp["ffn_w_down"][l]
        for i in range(11):
            blk = np.zeros((D, 2, 256), f)
            c0, c1 = 256 * i, min(256 * i + 256, DFF)
            blk[:, 0, :c1 - c0] = Wup[:, c0:c1]
            blk[:, 1, :c1 - c0] = Wup[:, DFF + c0:DFF + c1]
            units[UIDX["UP%d_%d" % (l, i)]] = _unit(blk.reshape(D, 512))
        Wdp = np.zeros((NPC * 128, D), f)
        Wdp[:DFF] = Wd
        Wdp = Wdp.reshape(NPC, 128, D)
        for hf in range(2):
            for gi in range(3):
                blk = np.zeros((8, 128, 512), f)
                n = min(8, NPC - 8 * gi)
                blk[:n] = Wdp[8 * gi:8 * gi + n, :, 512 * hf:512 * hf + 512]
                units[UIDX["DN%d_%d_%d" % (l, hf, gi)]] = np.ascontiguousarray(blk.transpose(1, 0, 2)).reshape(128, USZ)
        units[UIDX["MKV%d" % l]] = _unit(inp["mem_w_kv"][l])
    for k in range(3):
        units[UIDX["KV%d" % k]] = _unit(kv_w[:, 512 * k:512 * k + 512])
    for k in range(2):
        blk = np.zeros((D, 6, 65), f)
        blk[:, :, :64] = b_w_q[:, 384 * k:384 * k + 384].reshape(D, 6, 64)
        units[UIDX["BQ%d" % k]] = _unit(blk.reshape(D, 390))
    units[UIDX["BQM"]] = _unit(b_w_q[:, 768:1024])

    cols = np.zeros((128, NCOLS), f)
    for l in range(2):
        for nm, key in (("ln1g", "ln1_g"), ("ln1b", "ln1_b"), ("ln2g", "ln2_g"), ("ln2b", "ln2_b")):
            cols[:, COLMAP[(nm, l)]:COLMAP[(nm, l)] + 8] = inp[key][l].reshape(8, 128).T
        cwp = np.zeros((3, 2, NPC * 128), f)
        cwp[:, :, :DFF] = inp["ffn_conv_w"][l].reshape(3, 2, DFF)
        cbp = np.zeros((2, NPC * 128), f)
        cbp[:, :DFF] = inp["ffn_conv_b"][l].reshape(2, DFF)
        for k in range(3):
            cols[:, COLMAP[("cw", l, k)]:COLMAP[("cw", l, k)] + 44] = cwp[k].reshape(44, 128).T
        cols[:, COLMAP[("cb", l)]:COLMAP[("cb", l)] + 44] = cbp.reshape(44, 128).T
    cols[:, COLMAP["pscale"]:COLMAP["pscale"] + 6] = inp["a_pool_scale"][0].reshape(6, 128).T

    fbc = np.ascontiguousarray(np.broadcast_to(inp["f_b"].astype(f)[None, :], (128, NH)))

    consts = np.zeros((128, CW), f)
    ii = np.arange(128)
    consts[:, C_U:C_U + 128] = (ii[:, None] <= ii[None, :]).astype(f)
    consts[:, C_ID:C_ID + 128] = np.eye(128, dtype=f)
    consts[:, C_MASK:C_MASK + 128] = np.where(ii[:, None] > ii[None, :], NEG, 0.0).astype(f)
    for wi, w in enumerate((2, 4, 8, 16)):
        consts[:, C_INVC + 16 * wi:C_INVC + 16 * wi + 16] = (1.0 / np.minimum(np.arange(16) + 1, w))[None, :]
    for h in range(NH):
        consts[h, C_E + 65 * h + 64] = 8.0
    ss, tt = ii[:, None], ii[None, :]
    for wi, w in enumerate((2, 4, 8, 16)):
        main = np.where((ss <= tt) & (ss > tt - w), 1.0 / w, 0.0) - (ss == tt)
        halo = np.where(ss >= tt - w + 129, 1.0 / w, 0.0)
        first = np.where((ss <= tt) & (ss > tt - w), 1.0 / np.minimum(tt + 1, w), 0.0) - (ss == tt)
        for kind, m in enumerate((main, halo, first)):
            consts[:, C_B + (wi * 3 + kind) * 128:C_B + (wi * 3 + kind) * 128 + 128] = m.astype(f)

    pw = inp["a_pool_w"][0]
    wfull = np.zeros((768, 768), f)
    for g in range(4):
        wfull[192 * g:192 * g + 192, 192 * g:192 * g + 192] = pw[g]
    tiles_ = []
    for oc, srcs in {0: [0, 1], 1: [0, 1, 2], 2: [1, 2], 3: [3, 4], 4: [3, 4, 5], 5: [4, 5]}.items():
        for ch in srcs:
            tiles_.append(wfull[128 * ch:128 * ch + 128, 128 * oc:128 * oc + 128])
    wp = np.ascontiguousarray(np.stack(tiles_, axis=1)).reshape(128, 14 * 128)
    wfh = np.ascontiguousarray(kv_w[:, 1536:1548].reshape(KC, 128, NH).transpose(1, 0, 2)).reshape(128, KC * NH)
    return dict(wunits=units, cols=cols, fbc=fbc, consts=consts, wp=wp, wf=wfh)


def _to_fm(a):
    n, t, _ = a.shape
    return np.ascontiguousarray(a.reshape(n, t, KC, 128).transpose(0, 3, 2, 1))


def _from_fm(a):
    n, _, _, t = a.shape
    return np.ascontiguousarray(a.transpose(0, 3, 2, 1)).reshape(n, t, D)


_CACHE = {}


def run(inputs, n_cores, nseq, S, debug=False):
    inp = {k: np.asarray(v, dtype=np.float32) for k, v in inputs.items()}
    shared = _host_prepare(inp)
    key = (nseq, S, debug)
    if key not in _CACHE:
        _CACHE[key] = build_program(nseq, S, debug)[0]
    nc = _CACHE[key]
    in_maps = []
    for c in range(n_cores):
        m = dict(shared)
        m["xT"] = _to_fm(inp["x"][c * nseq:(c + 1) * nseq])
        m["memT"] = _to_fm(inp["mem"][c * nseq:(c + 1) * nseq])
        in_maps.append(m)
    res = run_bass_kernel_spmd(nc, in_maps, core_ids=list(range(n_cores)))
    out = np.concatenate([_from_fm(r["outT"]) for r in res.results], axis=0)
    if debug:
        dbg = np.concatenate([_from_fm(r["dbgT"]) for r in res.results], axis=0)
        return out, dbg
    return out


def kernel(**inputs):
    B, S, _ = inputs["x"].shape
    n_cores = 8
    return run(inputs, n_cores, B // n_cores, S).astype(np.float32)
```

```python
import numpy as np
import ml_dtypes
from contextlib import ExitStack
import concourse.bass as bass
import concourse.mybir as mybir
from concourse.bass_utils import run_bass_kernel_spmd

F32 = mybir.dt.float32
BF16 = mybir.dt.bfloat16
AF = mybir.ActivationFunctionType
ALU = mybir.AluOpType

D = 1024
KC = 8
T = 512
DFF = 2752
NPC = 22
NH = 12
MEM = 256
ALPHA = float((2.0 * 2) ** 0.25)
EPS = 1e-5
USZ = 4096
RING = 4
NEG = -30000.0

UNIT_NAMES = (["AIN0", "AIN1", "AOUT0", "AOUT1"] + ["UP0_%d" % i for i in range(11)]
              + ["DN0_%d_%d" % (hf, gi) for hf in range(2) for gi in range(3)]
              + ["KV0", "KV1", "KV2", "BQ0", "BQ1", "BQM", "BOUT0", "BOUT1"]
              + ["UP1_%d" % i for i in range(11)]
              + ["DN1_%d_%d" % (hf, gi) for hf in range(2) for gi in range(3)]
              + ["MKV0", "MKV1"])
UIDX = {n: i for i, n in enumerate(UNIT_NAMES)}
NU = len(UNIT_NAMES)


def unit_ncols(name):
    if name.startswith("BQ") and name != "BQM":
        return 390
    if name == "BQM":
        return 256
    return 512


def _cols_layout():
    m = {}
    n = 0
    for l in range(2):
        for nm in ("ln1g", "ln1b", "ln2g", "ln2b"):
            m[(nm, l)] = n
            n += 8
    m["pscale"] = n
    n += 6
    for l in range(2):
        for k in range(3):
            m[("cw", l, k)] = n
            n += 44
        m[("cb", l)] = n
        n += 44
    return m, n


COLMAP, NCOLS = _cols_layout()
C_U, C_ID, C_MASK, C_INVC, C_E = 0, 128, 256, 384, 448
C_B = 448 + 780
CW = C_B + 12 * 128


class Sched:
    def __init__(self):
        self.ops = []
        self.res = {}

    def add(self, eng, fn, reads=(), writes=(), dsem=None):
        idx = len(self.ops)
        deps = set()
        for r in reads:
            st = self.res.get(r)
            if st is not None and st[0] is not None:
                deps.add(st[0])
            if st is not None and isinstance(r, tuple) and r[0] == "ps":
                for (e2, _d), ix in st[1].items():
                    if e2 != eng:
                        deps.add(ix)
        for w in writes:
            st = self.res.get(w)
            if st is not None:
                if st[0] is not None:
                    deps.add(st[0])
                deps.update(st[1].values())
        deps.discard(idx)
        self.ops.append(dict(eng=eng, fn=fn, deps=sorted(deps), dsem=dsem, needed=False, val=None))
        for r in reads:
            st = self.res.setdefault(r, [None, {}])
            st[1][(eng, dsem)] = idx
        for w in writes:
            self.res[w] = [idx, {}]
        return idx

    def emit(self, nc, stack):
        ops = self.ops
        for op in ops:
            for d in op["deps"]:
                ops[d]["needed"] = True
        esem = {}
        for e in ("pe", "act", "dve", "pool"):
            esem[e] = stack.enter_context(nc.semaphore("s_" + e))
        dsem = {}
        cnt = {e: 0 for e in esem}
        dcnt = {}
        for op in ops:
            if op["dsem"] is not None:
                if op["dsem"] not in dsem:
                    dsem[op["dsem"]] = stack.enter_context(nc.semaphore("d_" + str(op["dsem"])))
                    dcnt[op["dsem"]] = 0
                dcnt[op["dsem"]] += 1
                op["val"] = 16 * dcnt[op["dsem"]]
            elif op["needed"]:
                cnt[op["eng"]] += 1
                op["val"] = cnt[op["eng"]]
        self.stats = dict(cnt=cnt, nops=len(ops))
        by_eng = {e: [] for e in ("pe", "act", "dve", "pool", "sp")}
        for i, op in enumerate(ops):
            by_eng[op["eng"]].append(i)

        def run(ename, e):
            seen = {}
            for i in by_eng[ename]:
                op = ops[i]
                need = {}
                for d in op["deps"]:
                    dop = ops[d]
                    if dop["dsem"] is not None:
                        key = ("d", dop["dsem"])
                    else:
                        key = ("e", dop["eng"])
                        if dop["eng"] == "pe" and ename == "pe":
                            continue
                    if dop["val"] > need.get(key, 0):
                        need[key] = dop["val"]
                for key, v in need.items():
                    if seen.get(key, 0) >= v:
                        continue
                    seen[key] = v
                    s = dsem[key[1]] if key[0] == "d" else esem[key[1]]
                    e.wait_ge(s, v)
                ins = op["fn"](e)
                if op["dsem"] is not None:
                    ins.then_inc(dsem[op["dsem"]], 16)
                elif op["needed"]:
                    ins.then_inc(esem[ename], 1)
            if ename == "sp":
                for k, s in dsem.items():
                    e.wait_ge(s, 16 * dcnt[k])

        with nc.Block() as block:
            @block.sync
            def _(e):
                run("sp", e)

            @block.tensor
            def _(e):
                run("pe", e)

            @block.scalar
            def _(e):
                run("act", e)

            @block.vector
            def _(e):
                run("dve", e)

            @block.gpsimd
            def _(e):
                run("pool", e)


class Rot:
    def __init__(self, name, n):
        self.name, self.n, self.i = name, n, -1

    def next(self):
        self.i = (self.i + 1) % self.n
        return self.i, (self.name, self.i)


def build_program(NSEQ, S, debug=False):
    NT = S // T
    NKT = S // 128
    nc = bass.Bass("TRN2", target_bir_lowering=False)
    stack = ExitStack()
    sch = Sched()

    def dram(name, shape, dt, kind):
        return nc.dram_tensor(name, list(shape), dt, kind=kind)

    xT = dram("xT", [NSEQ, 128, KC, S], F32, "ExternalInput")
    memT = dram("memT", [NSEQ, 128, KC, MEM], F32, "ExternalInput")
    wunits = dram("wunits", [NU, 128, USZ], F32, "ExternalInput")
    colsd = dram("cols", [128, NCOLS], F32, "ExternalInput")
    fbcd = dram("fbc", [128, NH], F32, "ExternalInput")
    constd = dram("consts", [128, CW], F32, "ExternalInput")
    wpd = dram("wp", [128, 14 * 128], F32, "ExternalInput")
    wfd = dram("wf", [128, KC * NH], F32, "ExternalInput")
    outT = dram("outT", [NSEQ, 128, KC, S], F32, "ExternalOutput")
    dbgT = dram("dbgT", [NSEQ, 128, KC, S], F32, "ExternalOutput") if debug else None
    wb = dram("wb", [NU, 128, USZ], BF16, "Internal")
    kcd = dram("kcache", [NSEQ, NH * 64, S], BF16, "Internal")
    vcd = dram("vcache", [NSEQ, S, NH * 64], BF16, "Internal")

    def sb(name, shape, dt):
        return stack.enter_context(nc.sbuf_tensor(name, list(shape), dt))

    wring = sb("wring", [128, RING, USZ], BF16)
    xr = sb("xr", [128, NSEQ, KC, T], F32)
    xb = sb("xb", [128, NSEQ, KC, T], BF16)
    scr = sb("scr", [128, NPC, T], BF16)
    utok = sb("utok", [128, 4, 768], BF16)
    uprev = sb("uprev", [128, NSEQ, 768], BF16)
    pT = sb("pT", [128, 4, T], BF16)
    rden = sb("rden", [128, 2, T], F32)
    tb = sb("tb", [128, 4, T], BF16)
    st = sb("st", [128, 4, T], F32)
    tf = sb("tf", [128, 2, T], F32)
    acc = sb("acc", [128, 4, T], F32)
    sg = sb("sg", [128, 2, T], BF16)
    chal = sb("chal", [128, NSEQ, 2, 44, 2], F32)
    hb = sb("hb", [128, 44, 2], F32)
    kt2 = sb("kt2", [128, 2, T], BF16)
    vt = sb("vt", [128, 2, 768], BF16)
    kst = sb("kst", [65, 3, 1024], BF16)
    vst = sb("vst", [128, 3, 8, 128], BF16)
    nFres = sb("nFres", [128, NSEQ, NKT, NH], F32)
    zt = sb("zt", [128, 4, NH], F32)
    lt = sb("lt", [128, 4, NH], F32)
    lb = sb("lb", [128, 4, NH], BF16)
    facc = sb("facc", [128, NSEQ, NH], F32)
    faccb = sb("faccb", [128, NSEQ, NH], BF16)
    ffm = sb("ffm", [NH, T], BF16)
    memKT = sb("memKT", [128, NSEQ, 2, 2, MEM], BF16)
    memV = sb("memV", [128, NSEQ, 2, 2, 4, 128], BF16)
    qmz = sb("qmz", [128, 4, T], BF16)
    memTb = sb("memTb", [128, KC, MEM], BF16)
    cols = sb("colsb", [128, NCOLS], F32)
    fbc = sb("fbcb", [128, NH], F32)
    cf = sb("cf", [128, 192], F32)
    cb = sb("cbf", [128, CW], BF16)
    wp = sb("wp_b", [128, 14 * 128], BF16)
    wf = sb("wf_b", [128, KC * NH], BF16)
    ones_b = sb("ones_b", [128, 128], BF16)
    ones_f = sb("ones_f", [128, 128], F32)
    epsc = sb("epsc", [128, 1], F32)
    onec = sb("onec", [128, 1], F32)
    ps = stack.enter_context(nc.psum_tensor("ps", [128, 8, T], F32))

    held = set()
    bank_ptr = [0]

    def nb():
        for _ in range(16):
            b = bank_ptr[0]
            bank_ptr[0] = (b + 1) % 8
            if b not in held:
                return b
        raise RuntimeError("no psum bank")

    def PS(b):
        return ("ps", b)

    def col(idx):
        return cols[:, idx:idx + 1]

    def pe_group(mms, reads, writes):
        def fn(e, mms=mms):
            ins = None
            for (o, l, r, s0, s1) in mms:
                ins = e.matmul(o, l, r, start=s0, stop=s1)
            return ins
        return sch.add("pe", fn, reads, writes)

    def act(out, in_, func, reads, writes, bias=None, scale=None):
        kw = {}
        if bias is not None:
            kw["bias"] = bias
        if scale is not None:
            kw["scale"] = scale
        return sch.add("act", lambda e: e.activation(out, in_, func, **kw), reads, writes)

    def dve(f, reads, writes):
        return sch.add("dve", f, reads, writes)

    def pool(f, reads, writes):
        return sch.add("pool", f, reads, writes)

    def dma(eng, out, in_, reads, writes, dsem):
        return sch.add(eng, lambda e: e.dma_start(out=out, in_=in_), reads, writes, dsem=dsem)

    order = []

    class Ring:
        def __init__(self):
            self.plan = []
            self.seen = set()
            self.issued = 0
            self.consumed = 0

        def set_plan(self, plan):
            self.plan = plan

        def _issue(self):
            k = self.issued
            name = self.plan[k]
            u = UIDX[name]
            n = KC * unit_ncols(name)
            slot = k % RING
            if name not in self.seen:
                self.seen.add(name)
                dma("pool", wring[:, slot, 0:n], wunits[u, :, 0:n], [], [("w", slot)], ("w", slot))
                dma("sp", wb[u, :, 0:n], wring[:, slot, 0:n], [("w", slot)], [("wb", u)], ("wbst", slot))
            else:
                dma("sp", wring[:, slot, 0:n], wb[u, :, 0:n], [("wb", u)], [("w", slot)], ("w", slot))
            self.issued += 1

        def get(self, name):
            k = self.consumed
            assert self.plan[k] == name, (self.plan[k], name)
            while self.issued < min(len(self.plan), k + RING - 1):
                self._issue()
            self.consumed += 1
            slot = k % RING
            return slot

    ring = Ring()
    plan = []
    for z in range(NSEQ):
        plan += ["MKV0", "MKV1"]
    M0U = ["AIN0", "AIN1", "AOUT0", "AOUT1"]
    F0U = ["UP0_%d" % i for i in range(11)] + ["DN0_%d_%d" % (hf, gi) for hf in range(2) for gi in range(3)]
    M1U = ["KV0", "KV1", "KV2", "BQ0", "BQ1", "BQM", "BOUT0", "BOUT1"]
    F1U = ["UP1_%d" % i for i in range(11)] + ["DN1_%d_%d" % (hf, gi) for hf in range(2) for gi in range(3)]
    for i in range(NT):
        for grp in (M0U, F0U, M1U, F1U):
            for z in range(NSEQ):
                plan += grp
    ring.set_plan(plan)

    def W(slot, ncols, kc, c0, c1, p0=0, p1=128):
        return wring[p0:p1, slot, kc * ncols + c0: kc * ncols + c1]

    dma("sp", cols[:, :], colsd[:, :], [], ["cols"], "i0")
    dma("sp", fbc[:, :], fbcd[:, :], [], ["fbc"], "i1")
    dma("sp", cf[:, 0:128], constd[:, 0:128], [], ["cf"], "i2")
    dma("sp", cf[:, 128:192], constd[:, C_INVC:C_INVC + 64], ["cf"], ["cf"], "i2b")
    dma("pool", cb[:, :], constd[:, :], [], ["cb"], "i3")
    dma("pool", wp[:, :], wpd[:, :], [], ["wp"], "i4")
    dma("pool", wf[:, :], wfd[:, :], [], ["wf"], "i5")
    pool(lambda e: e.memset(ones_b[:, :], 1.0), [], ["ones_b"])
    pool(lambda e: e.memset(ones_f[:, :], 1.0), [], ["ones_f"])
    pool(lambda e: e.memset(epsc[:, :], EPS), [], ["epsc"])
    pool(lambda e: e.memset(onec[:, :], 1.0), [], ["onec"])
    pool(lambda e: e.memset(kst[:, :, :], 1.0), [], [("kst", 0), ("kst", 1), ("kst", 2)])
    pool(lambda e: e.memset(vst[:, :, :, :], 1.0), [], [("vst", 0), ("vst", 1), ("vst", 2)])
    for z_ in range(NSEQ):
        for l_ in range(2):
            for m_ in range(2):
                pool(lambda e, z_=z_, l_=l_, m_=m_: e.memset(memV[:, z_, l_, m_, :, :], 1.0), [("memV", z_, l_)], [("memV", z_, l_)])
    pool(lambda e: e.memset(qmz[:, :, :], 0.0), [], [("qmz", h) for h in range(4)])

    def H(j, p1=128):
        return scr[0:p1, j, :], ("S", j)

    def MIX(c, p0=0, p1=128):
        return scr[p0:p1, c, :], ("S", c)

    def POOLED(c, p0=0, p1=128):
        return scr[p0:p1, 8 + c, :], ("S", 8 + c)

    def QAUG(h, c0=0, c1=T):
        return scr[0:65, 8 + h, c0:c1], ("S", 8 + h)

    def QM(j, p0=0, p1=128):
        return scr[p0:p1, 20 + j, :], ("S", 20 + j)

    cur = [0]

    def XRN(c, p=None):
        return ("xr", cur[0] if p is None else p, c)

    def XRV(c, p=None):
        return xr[:, cur[0] if p is None else p, c, :]

    def XBN(c):
        return ("xb", cur[0], c)

    def XBL():
        return [("xb", cur[0], c) for c in range(KC)]


    tbrot = Rot("tb", 4)
    tfrot = Rot("tf", 2)
    ln_state = {}

    late = []

    def drain_late(n=None, stream=None):
        saved = cur[0]
        if stream is not None:
            keep = []
            for (z, f) in list(late):
                if z == stream:
                    cur[0] = z
                    f()
                else:
                    keep.append((z, f))
            late[:] = keep
        else:
            k = len(late) if n is None else min(n, len(late))
            for _ in range(k):
                z, f = late.pop(0)
                cur[0] = z
                f()
        cur[0] = saved

    def ln_begin(drain=True):
        if drain:
            drain_late()
        ln_state["b"] = None
        ln_state["n"] = 0

    def ln_banks():
        if ln_state["b"] is None:
            b1 = nb()
            held.add(b1)
            b2 = nb()
            held.add(b2)
            ln_state["b"] = (b1, b2)
        return ln_state["b"]

    def res_chunk(oc, bank):
        xv, xn = XRV(oc), XRN(oc)
        dve(lambda e: e.scalar_tensor_tensor(out=xv, in0=xv, scalar=ALPHA,
                                             in1=ps[:, bank, :], op0=ALU.mult, op1=ALU.add),
            [PS(bank), xn], [xn])
        i1, n1 = tbrot.next()
        act(tb[:, i1, :], xv, AF.Copy, [xn], [n1])
        i2, n2 = tbrot.next()
        act(tb[:, i2, :], xv, AF.Square, [xn], [n2])
        ln_state.setdefault("pend", []).append((i1, n1, i2, n2))

    def stat_flush(keep=0):
        pend = ln_state.setdefault("pend", [])
        if len(pend) <= keep:
            return
        b1, b2 = ln_banks()
        while len(pend) > keep:
            i1, n1, i2, n2 = pend.pop(0)
            k = ln_state["n"]
            ln_state["n"] += 1
            pe_group([(ps[:, b1, :], ones_b[:, :], tb[:, i1, :], k == 0, k == KC - 1),
                      (ps[:, b2, :], ones_b[:, :], tb[:, i2, :], k == 0, k == KC - 1)],
                     [n1, n2, "ones_b"], [PS(b1), PS(b2)])

    def ln_finish(gkey, bkey, l, write_xb=True):
        stat_flush(0)
        b1, b2 = ln_banks()
        mean, var, rstd, nmr = st[:, 0, :], st[:, 1, :], st[:, 2, :], st[:, 3, :]
        act(mean, ps[:, b1, :], AF.Copy, [PS(b1)], [("st", 0)], scale=1.0 / D)
        dve(lambda e: e.tensor_tensor(out=var, in0=mean, in1=mean, op=ALU.mult), [("st", 0)], [("st", 1)])
        dve(lambda e: e.scalar_tensor_tensor(out=var, in0=ps[:, b2, :], scalar=1.0 / D, in1=var,
                                             op0=ALU.mult, op1=ALU.subtract),
            [PS(b2), ("st", 1)], [("st", 1)])
        held.discard(b1)
        held.discard(b2)
        z0 = cur[0]

        def tail_head():
            act(rstd, var, AF.Ln, [("st", 1), "epsc"], [("st", 2)], bias=epsc[:, 0:1], scale=1.0)
            act(rstd, rstd, AF.Exp, [("st", 2)], [("st", 2)], scale=-0.5)
            dve(lambda e: e.scalar_tensor_tensor(out=nmr, in0=mean, scalar=-1.0, in1=rstd,
                                                 op0=ALU.mult, op1=ALU.mult),
                [("st", 0), ("st", 2)], [("st", 3)])
        late.append((z0, tail_head))
        g0 = COLMAP[(gkey, l)]
        b0 = COLMAP[(bkey, l)]

        def chunk(c):
            i, n = tfrot.next()
            xv, xn = XRV(c), XRN(c)
            dve(lambda e, i=i, xv=xv: e.tensor_tensor(out=tf[:, i, :], in0=xv, in1=rstd, op=ALU.mult),
                [xn, ("st", 2)], [n])
            dve(lambda e, i=i: e.tensor_tensor(out=tf[:, i, :], in0=tf[:, i, :], in1=nmr, op=ALU.add),
                [n, ("st", 3)], [n])
            if write_xb:
                act(xb[:, cur[0], c, :], tf[:, i, :], AF.Identity, [n, "cols"], [XBN(c)],
                    bias=col(b0 + c), scale=col(g0 + c))
            act(xv, tf[:, i, :], AF.Identity, [n, "cols"], [xn],
                bias=col(b0 + c), scale=col(g0 + c))
        for c in range(KC):
            late.append((z0, (lambda c=c: chunk(c))))

    ptrot = Rot("pT", 4)
    rdrot = Rot("rden", 2)

    def qm_evac(j, bank):
        act(qmz[0:64, 2 * j, :], ps[0:64, bank, :], AF.Copy, [PS(bank)], [("qmz", 2 * j)])
        act(qmz[64:128, 2 * j + 1, :], ps[64:128, bank, :], AF.Copy, [PS(bank)], [("qmz", 2 * j + 1)])

    def mem_attention(l):
        steps = [(h, mc) for h in range(4) for mc in range(2)]
        stt = {}

        def qk(h, mc):
            if mc == 0:
                bh = nb()
                held.add(bh)
                stt[("b", h)] = bh
            bl = nb()
            pe_group([(ps[:, bl, :], memKT[:, cur[0], l, h // 2, mc * 128:(mc + 1) * 128], qmz[:, h, :], True, True)],
                     [("qmz", h), ("memKT", cur[0], l)], [PS(bl)])
            pi, pn = ptrot.next()
            act(pT[:, pi, :], ps[:, bl, :], AF.Exp, [PS(bl)], [pn], scale=0.125)
            stt[(h, mc)] = (pi, pn)

        def pv(h, mc):
            pi, pn = stt.pop((h, mc))
            bh = stt[("b", h)]
            pe_group([(ps[:, bh, :], memV[:, cur[0], l, mc, h, :], pT[:, pi, :], mc == 0, mc == 1)],
                     [pn, ("memV", cur[0], l)], [PS(bh)])
            if mc == 1:
                hp = 64 * (h % 2)
                ri, rn = rdrot.next()
                act(rden[0:64, ri, :], ps[64:128, bh, :], AF.Ln, [PS(bh)], [rn])
                act(rden[0:64, ri, :], rden[0:64, ri, :], AF.Exp, [rn], [rn], scale=-1.0)
                mv, mn = MIX(6 + h // 2, hp, hp + 64)
                dve(lambda e, ri=ri, bh=bh, mv=mv: e.tensor_tensor(out=mv, in0=ps[0:64, bh, :], in1=rden[0:64, ri, :], op=ALU.mult),
                    [PS(bh), rn, mn], [mn])
                held.discard(bh)

        SK = 2
        for n in range(len(steps) + SK):
            if n < len(steps):
                qk(*steps[n])
            if n - SK >= 0:
                pv(*steps[n - SK])

    def wout_ln(unames, l):
        ln_begin()
        for oc in range(KC):
            if oc % 4 == 0:
                slot = ring.get(unames[oc // 4])
            b = nb()
            mms = []
            for kc in range(KC):
                mv, mn = MIX(kc)
                mms.append((ps[:, b, :], W(slot, 512, kc, (oc % 4) * 128, (oc % 4) * 128 + 128), mv, kc == 0, kc == KC - 1))
            pe_group(mms, [("w", slot)] + [("S", c) for c in range(KC)], [PS(b)])
            res_chunk(oc, b)
            stat_flush(1)
        ln_finish("ln1g", "ln1b", l)

    accrot = Rot("acc", 4)
    sgrot = Rot("sg", 2)

    def ffn(l, after_up=None, after_lnbegin=None):
        cw = [COLMAP[("cw", l, k)] for k in range(3)]
        cbi = COLMAP[("cb", l)]
        halr = [("chal", cur[0], l, ci) for ci in range(44)]
        h0, h1 = chal[:, cur[0], l, :, 0], chal[:, cur[0], l, :, 1]
        pool(lambda e: e.tensor_tensor(out=hb[:, :, 0], in0=h1, in1=cols[:, cw[1]:cw[1] + 44], op=ALU.mult), halr + ["cols"], ["hb"])
        pool(lambda e: e.tensor_tensor(out=hb[:, :, 1], in0=h0, in1=cols[:, cw[0]:cw[0] + 44], op=ALU.mult), halr + ["cols", "hb"], ["hb"])
        pool(lambda e: e.tensor_tensor(out=hb[:, :, 0], in0=hb[:, :, 0], in1=hb[:, :, 1], op=ALU.add), ["hb"], ["hb"])
        pool(lambda e: e.tensor_tensor(out=hb[:, :, 1], in0=h1, in1=cols[:, cw[0]:cw[0] + 44], op=ALU.mult), halr + ["cols", "hb"], ["hb"])
        gate_pending = []
        for i in range(11):
            slot = ring.get("UP%d_%d" % (l, i))
            for jj in range(2):
                j = 2 * i + jj
                M = 128
                br = []
                for ug in range(2):
                    ci = ug * 22 + j
                    b = nb()
                    mms = []
                    for kc in range(KC):
                        mms.append((ps[0:M, b, :], wring[:, slot, kc * 512 + ug * 256 + jj * 128: kc * 512 + ug * 256 + jj * 128 + M],
                                    xb[:, cur[0], kc, :], kc == 0, kc == KC - 1))
                    pe_group(mms, [("w", slot)] + XBL(), [PS(b)])
                    ai, an = accrot.next()
                    br.append((ci, b, ai, an))
                for (ci, b, ai, an) in br:
                    act(chal[0:M, cur[0], l, ci, 0:2], ps[0:M, b, T - 2:T], AF.Copy, [PS(b)], [("chal", cur[0], l, ci)])
                    act(acc[0:M, ai, :], ps[0:M, b, :], AF.Identity, [PS(b), "cols"], [an],
                        bias=cols[0:M, cbi + ci:cbi + ci + 1], scale=cols[0:M, cw[2] + ci:cw[2] + ci + 1])
                for (ci, b, ai, an) in br:
                    w1 = cols[0:M, cw[1] + ci:cw[1] + ci + 1]
                    dve(lambda e, ai=ai, b=b, M=M, w1=w1: e.scalar_tensor_tensor(
                        out=acc[0:M, ai, 1:T], in0=ps[0:M, b, 0:T - 1], scalar=w1, in1=acc[0:M, ai, 1:T],
                        op0=ALU.mult, op1=ALU.add), [PS(b), an, "cols"], [an])
                for (ci, b, ai, an) in br:
                    w0 = cols[0:M, cw[0] + ci:cw[0] + ci + 1]
                    dve(lambda e, ai=ai, b=b, M=M, w0=w0: e.scalar_tensor_tensor(
                        out=acc[0:M, ai, 2:T], in0=ps[0:M, b, 0:T - 2], scalar=w0, in1=acc[0:M, ai, 2:T],
                        op0=ALU.mult, op1=ALU.add), [PS(b), an, "cols"], [an])
                for (ci, b, ai, an) in br:
                    dve(lambda e, ai=ai, M=M, ci=ci: e.tensor_tensor(out=acc[0:M, ai, 0:2], in0=acc[0:M, ai, 0:2],
                                                                     in1=hb[0:M, ci, :], op=ALU.add), ["hb", an], [an])
                if gate_pending:
                    gate_pending.pop(0)()

                def gate(br=br, j=j, M=M):
                    (_, _, au, aun), (_, _, ag, agn) = br
                    si, sn = sgrot.next()
                    act(sg[0:M, si, :], acc[0:M, ag, :], AF.Silu, [agn], [sn])
                    hv, hn2 = H(j, M)
                    pool(lambda e, hv=hv, si=si, au=au, M=M: e.tensor_tensor(out=hv, in0=sg[0:M, si, :], in1=acc[0:M, au, :], op=ALU.mult),
                         [sn, aun], [hn2])
                gate_pending.append(gate)
        while gate_pending:
            gate_pending.pop(0)()
        if after_up is not None:
            after_up()
        ln_begin(drain=(after_lnbegin is not None))
        if after_lnbegin is not None:
            after_lnbegin()
        for hf in range(2):
            banks = []
            for o in range(4):
                b = nb()
                held.add(b)
                banks.append(b)
            for gi in range(3):
                slot = ring.get("DN%d_%d_%d" % (l, hf, gi))
                mms = []
                reads = [("w", slot)]
                for pair in ((0, 1), (2, 3)):
                    mms = []
                    reads = [("w", slot)]
                    for jl in range(8):
                        j = 8 * gi + jl
                        if j >= NPC:
                            break
                        K = 128
                        reads.append(("S", j))
                        for o in pair:
                            mms.append((ps[:, banks[o], :], wring[0:K, slot, jl * 512 + o * 128: jl * 512 + o * 128 + 128],
                                        scr[0:K, j, :], j == 0, j == NPC - 1))
                    pe_group(mms, reads, [PS(banks[o]) for o in pair])
            for o in range(4):
                held.discard(banks[o])
                res_chunk(4 * hf + o, banks[o])
                stat_flush(1)
            if hf == 0:
                drain_late()
        ln_finish("ln2g", "ln2b", l, write_xb=(l == 0))


    def seq_setup(s):
        pool(lambda e: e.memset(chal[:, s, :, :, :], 0.0), [], [("chal", s, l, ci) for l in range(2) for ci in range(44)])
        pool(lambda e: e.memset(facc[:, s, :], 0.0), [], [("facc", s)])
        pool(lambda e: e.memset(faccb[:, s, :], 0.0), [], [("faccb", s)])
        dma("pool", memTb[:, :, :], memT[s, :, :, :], [], [("memTb", kc) for kc in range(KC)], "memTb")
        for l in range(2):
            slot = ring.get("MKV%d" % l)
            for j in range(2):
                b = nb()
                mms = [(ps[:, b, 0:MEM], W(slot, 512, kc, j * 128, j * 128 + 128), memTb[:, kc, :], kc == 0, kc == KC - 1)
                       for kc in range(KC)]
                pe_group(mms, [("w", slot)] + [("memTb", kc) for kc in range(KC)], [PS(b)])
                act(memKT[:, s, l, j, :], ps[:, b, 0:MEM], AF.Copy, [PS(b)], [("memKT", s, l)])
            for mc in range(2):
                b = nb()
                mms = [(ps[:, b, 0:MEM], memTb[:, kc, mc * 128:(mc + 1) * 128], W(slot, 512, kc, 256, 512), kc == 0, kc == KC - 1)
                       for kc in range(KC)]
                pe_group(mms, [("w", slot)] + [("memTb", kc) for kc in range(KC)], [PS(b)])
                for h in range(4):
                    act(memV[:, s, l, mc, h, 0:64], ps[:, b, 64 * h:64 * h + 64], AF.Copy, [PS(b), ("memV", s, l)], [("memV", s, l)])

    WIN = {0: [(0, 128, 2)], 1: [(0, 64, 2), (64, 128, 4)], 2: [(0, 128, 4)], 3: [(0, 128, 8)],
           4: [(0, 64, 8), (64, 128, 16)], 5: [(0, 128, 16)]}

    def layer0_mixer(first):
        z = cur[0]
        s_in = [ring.get("AIN0"), ring.get("AIN1")]
        for sub in range(4):
            b1 = nb()
            b2 = nb()
            tk = xb[:, z, :, sub * 128:(sub + 1) * 128]
            mms = [(ps[:, b1, :], xb[:, z, kc, sub * 128:(sub + 1) * 128], W(s_in[0], 512, kc, 0, 512), kc == 0, kc == KC - 1)
                   for kc in range(KC)]
            mms += [(ps[:, b2, 0:256], xb[:, z, kc, sub * 128:(sub + 1) * 128], W(s_in[1], 512, kc, 0, 256), kc == 0, kc == KC - 1)
                    for kc in range(KC)]
            pe_group(mms, [("w", s_in[0]), ("w", s_in[1])] + XBL(), [PS(b1), PS(b2)])
            drain_late(1)
            act(utok[:, sub, 0:512], ps[:, b1, :], AF.Copy, [PS(b1)], [("utok", sub)])
            act(utok[:, sub, 512:768], ps[:, b2, 0:256], AF.Copy, [PS(b2), ("utok", sub)], [("utok", sub)])
        for oc in (6, 7):
            b = nb()
            mms = [(ps[:, b, :], W(s_in[1], 512, kc, (oc % 4) * 128, (oc % 4) * 128 + 128), xb[:, z, kc, :], kc == 0, kc == KC - 1)
                   for kc in range(KC)]
            pe_group(mms, [("w", s_in[1])] + XBL(), [PS(b)])
            drain_late(1)
            qm_evac(oc - 6, b)
        for c in range(6):
            for (p0, p1, w) in WIN[c]:
                wi = {2: 0, 4: 1, 8: 2, 16: 3}[w]
                bmain = cb[:, C_B + (wi * 3 + 0) * 128:C_B + (wi * 3 + 0) * 128 + 128]
                bhalo = cb[:, C_B + (wi * 3 + 1) * 128:C_B + (wi * 3 + 1) * 128 + 128]
                bfirst = cb[:, C_B + (wi * 3 + 2) * 128:C_B + (wi * 3 + 2) * 128 + 128]
                bv = nb()
                mms = []
                reads = ["cb"] + [("utok", sub) for sub in range(4)]
                for sub in range(4):
                    o = ps[:, bv, sub * 128:(sub + 1) * 128]
                    cur_u = utok[:, sub, c * 128:(c + 1) * 128]
                    if sub == 0 and first:
                        mms.append((o, cur_u, bfirst, True, True))
                        continue
                    mms.append((o, cur_u, bmain, True, False))
                    if sub == 0:
                        mms.append((o, uprev[:, z, c * 128:(c + 1) * 128], bhalo, False, True))
                        reads.append(("uprev", z))
                    else:
                        mms.append((o, utok[:, sub - 1, c * 128:(c + 1) * 128], bhalo, False, True))
                pe_group(mms, reads, [PS(bv)])
                pv, pn = POOLED(c, p0, p1)
                act(pv, ps[p0:p1, bv, :], AF.Copy, [PS(bv), pn], [pn])
        dve(lambda e, z=z: e.tensor_copy(uprev[:, z, :], utok[:, 3, :]), [("utok", 3)], [("uprev", z)])

    GL_TILES = {0: [0, 1], 1: [0, 1, 2], 2: [1, 2], 3: [3, 4], 4: [3, 4, 5], 5: [4, 5]}

    def layer0_glinear():
        PSC = COLMAP["pscale"]
        ti = 0
        for oc in range(6):
            b = nb()
            mms = []
            reads = ["wp"]
            srcs = GL_TILES[oc]
            for k, ch in enumerate(srcs):
                pv, pn = POOLED(ch)
                reads.append(pn)
                mms.append((ps[:, b, :], wp[:, ti * 128:(ti + 1) * 128], pv, k == 0, k == len(srcs) - 1))
                ti += 1
            pe_group(mms, reads, [PS(b)])
            mv, mn = MIX(oc)
            act(mv, ps[:, b, :], AF.Identity, [PS(b), "cols"], [mn], scale=col(PSC + oc))

    ktrot = Rot("kt2", 2)
    vtrot = Rot("vt", 2)
    ztrot = Rot("zt", 4)

    def layer1_proj(s, i):
        t0 = i * T
        bfm = nb()
        held.add(bfm)
        fst = {}

        def stageA(sub):
            b = nb()
            mms = [(ps[:, b, 0:NH], xb[:, cur[0], kc, sub * 128:(sub + 1) * 128], wf[:, kc * NH:(kc + 1) * NH], kc == 0, kc == KC - 1)
                   for kc in range(KC)]
            pe_group(mms, ["wf"] + XBL(), [PS(b)])
            zi, zn = ztrot.next()
            ltn, lbn = ("lt", zi), ("lb", zi)
            dve(lambda e, zi=zi, b=b: e.tensor_tensor(out=zt[:, zi, :], in0=ps[:, b, 0:NH], in1=fbc[:, :], op=ALU.add),
                [PS(b), "fbc"], [zn])
            act(zt[:, zi, :], zt[:, zi, :], AF.Exp, [zn], [zn], scale=-1.0)
            act(lt[:, zi, :], zt[:, zi, :], AF.Ln, [zn, "onec"], [ltn], bias=onec[:, 0:1], scale=1.0)
            dve(lambda e, zi=zi: e.tensor_copy(lb[:, zi, :], lt[:, zi, :]), [ltn], [lbn])
            fst[sub] = (zi, ltn, lbn)

        def stageB(sub):
            zi, ltn, lbn = fst[sub]
            gsub = 4 * i + sub
            b2 = nb()
            pe_group([(ps[:, b2, 0:NH], cf[:, C_U:C_U + 128], lt[:, zi, :], True, False),
                      (ps[:, b2, 0:NH], ones_f[:, :], facc[:, cur[0], :], False, True)],
                     ["cf", ltn, "ones_f", ("facc", cur[0])], [PS(b2)])
            pe_group([(ps[0:NH, bfm, sub * 128:(sub + 1) * 128], lb[:, zi, :], cb[:, C_U:C_U + 128], True, False),
                      (ps[0:NH, bfm, sub * 128:(sub + 1) * 128], faccb[:, cur[0], :], ones_b[:, :], False, True)],
                     [lbn, "cb", ("faccb", cur[0]), "ones_b"], [PS(bfm)])
            dve(lambda e, b2=b2, gsub=gsub, z=cur[0]: e.tensor_copy(nFres[:, z, gsub, :], ps[:, b2, 0:NH]), [PS(b2)], [("nF", cur[0], gsub)])
            dve(lambda e, zi=zi, z=cur[0]: e.tensor_tensor(out=facc[:, z, :], in0=facc[:, z, :], in1=lt[:, zi, :], op=ALU.add),
                [ltn, ("facc", cur[0])], [("facc", cur[0])])
            dve(lambda e, z=cur[0]: e.tensor_copy(faccb[:, z, :], facc[:, z, :]), [("facc", cur[0])], [("faccb", cur[0])])

        s0 = ring.get("KV0")
        s1 = ring.get("KV1")

        def kgroup(j):
            slot, cofs = (s0, 128 * j) if j < 4 else (s1, 128 * (j - 4))
            b = nb()
            mms = [(ps[:, b, :], W(slot, 512, kc, cofs, cofs + 128), xb[:, cur[0], kc, :], kc == 0, kc == KC - 1) for kc in range(KC)]
            pe_group(mms, [("w", slot)] + XBL(), [PS(b)])
            drain_late(1)
            r, rn = ktrot.next()
            act(kt2[:, r, :], ps[:, b, :], AF.Copy, [PS(b)], [rn])
            dma("sp", kcd[s, 128 * j:128 * j + 128, t0:t0 + T], kt2[:, r, :], [rn], [("kc", s, j, i)], rn)

        def vgroup(sub, s2):
            b1 = nb()
            b2 = nb()
            mms = [(ps[:, b1, 0:256], xb[:, cur[0], kc, sub * 128:(sub + 1) * 128], W(s1, 512, kc, 256, 512), kc == 0, kc == KC - 1)
                   for kc in range(KC)]
            mms += [(ps[:, b2, :], xb[:, cur[0], kc, sub * 128:(sub + 1) * 128], W(s2, 512, kc, 0, 512), kc == 0, kc == KC - 1)
                    for kc in range(KC)]
            pe_group(mms, [("w", s1), ("w", s2)] + XBL(), [PS(b1), PS(b2)])
            drain_late(1)
            r, rn = vtrot.next()
            act(vt[:, r, 0:256], ps[:, b1, 0:256], AF.Copy, [PS(b1)], [rn])
            act(vt[:, r, 256:768], ps[:, b2, :], AF.Copy, [PS(b2), rn], [rn])
            dma("sp", vcd[s, t0 + sub * 128:t0 + sub * 128 + 128, :], vt[:, r, :], [rn], [("vc", s, i, sub)], rn)

        for sub in range(4):
            stageA(sub)
        kgroup(0)
        kgroup(1)
        stageB(0)
        kgroup(2)
        kgroup(3)
        stageB(1)
        kgroup(4)
        kgroup(5)
        stageB(2)
        s2 = ring.get("KV2")
        vgroup(0, s2)
        vgroup(1, s2)
        stageB(3)
        act(ffm[:, :], ps[0:NH, bfm, :], AF.Copy, [PS(bfm)], ["ffm"], scale=-1.0)
        held.discard(bfm)
        vgroup(2, s2)
        vgroup(3, s2)
        for h in range(NH):
            if h % 6 == 0:
                slot = ring.get("BQ%d" % (h // 6))
            b = nb()
            hc = (h % 6) * 65
            mms = [(ps[0:65, b, :], W(slot, 390, kc, hc, hc + 65), xb[:, cur[0], kc, :], kc == 0, False) for kc in range(KC)]
            mms.append((ps[0:65, b, :], cb[0:NH, C_E + 65 * h:C_E + 65 * h + 65], ffm[:, :], False, True))
            qv, qn = QAUG(h)
            pe_group(mms, [("w", slot), "cb", "ffm"] + XBL(), [PS(b)])
            act(qv, ps[0:65, b, :], AF.Copy, [PS(b)], [qn], scale=0.125)
        slot = ring.get("BQM")
        for j in range(2):
            b = nb()
            mms = [(ps[:, b, :], W(slot, 256, kc, j * 128, j * 128 + 128), xb[:, cur[0], kc, :], kc == 0, kc == KC - 1) for kc in range(KC)]
            pe_group(mms, [("w", slot)] + XBL(), [PS(b)])
            qm_evac(j, b)

    ksrot = Rot("kst", 3)
    vsrot = Rot("vst", 3)

    def fox_prefetch(s, i):
        nkt = 4 * (i + 1)
        n_here = min(8, nkt)
        nk = 128 * n_here
        ki, kn = ksrot.next()
        vi, vn = vsrot.next()
        kreads = [("kc", s, 0, ti) for ti in range(0, (nk - 1) // T + 1)]
        vreads = [("vc", s, a_ // 4, a_ % 4) for a_ in range(n_here)]
        dma("sp", kst[0:64, ki, 0:nk], kcd[s, 0:64, 0:nk], kreads, [kn], kn)
        vsrc = vcd.rearrange("s (kt p) d -> s p kt d", p=128)[s, :, 0:n_here, 0:64]
        dma("sp", vst[:, vi, 0:n_here, 0:64], vsrc, vreads, [vn], vn)
        return {(0, 0): (ki, kn, vi, vn)}

    def fox_attention(s, i, pre=None):
        nkt = 4 * (i + 1)
        for b in (6, 7):
            held.add(b)
        steps = []
        for h in range(NH):
            hp = 64 * (h % 2)
            ba = 6 + (h % 2)
            nch = (nkt + 7) // 8
            for ck in range(nch):
                kt0 = 8 * ck
                n_here = min(8, nkt - kt0)
                for kk in range(n_here):
                    steps.append((h, hp, ba, ck, kt0, n_here, kk))
        state = dict(pre or {})

        def qk(st_):
            h, hp, ba, ck, kt0, n_here, kk = st_
            if kk == 0 and (h, ck) not in state:
                nk = 128 * n_here
                k0 = 128 * kt0
                ki, kn = ksrot.next()
                vi, vn = vsrot.next()
                kreads = [("kc", s, h // 2, ti) for ti in range(k0 // T, (k0 + nk - 1) // T + 1)]
                vreads = [("vc", s, (kt0 + a_) // 4, (kt0 + a_) % 4) for a_ in range(n_here)]
                dma("sp", kst[0:64, ki, 0:nk], kcd[s, 64 * h:64 * h + 64, k0:k0 + nk], kreads, [kn], kn)
                vsrc = vcd.rearrange("s (kt p) d -> s p kt d", p=128)[s, :, kt0:kt0 + n_here, 64 * h:64 * h + 64]
                dma("sp", vst[:, vi, 0:n_here, 0:64], vsrc, vreads, [vn], vn)
                state[(h, ck)] = (ki, kn, vi, vn)
            ki, kn, vi, vn = state[(h, ck)]
            kt = kt0 + kk
            jj = kt - 4 * i
            q0 = 128 * jj if jj >= 0 else 0
            N = T - q0
            bl = nb()
            qv, qn = QAUG(h, q0, T)
            mms = [(ps[:, bl, 0:N], kst[0:65, ki, kk * 128:(kk + 1) * 128], qv, True, jj < 0)]
            reads = [kn, qn]
            if jj >= 0:
                mms.append((ps[:, bl, 0:128], cb[:, C_ID:C_ID + 128], cb[:, C_MASK:C_MASK + 128], False, True))
                reads.append("cb")
            pe_group(mms, reads, [PS(bl)])
            pi, pn = ptrot.next()
            act(pT[:, pi, 0:N], ps[:, bl, 0:N], AF.Exp, [PS(bl), ("nF", cur[0], kt)], [pn],
                bias=nFres[:, cur[0], kt, h:h + 1], scale=1.0)
            state[st_] = (pi, pn, q0, N, kt, vi, vn)

        def pv(st_):
            h, hp, ba, ck, kt0, n_here, kk = st_
            pi, pn, q0, N, kt, vi, vn = state.pop(st_)
            pe_group([(ps[:, ba, q0:T], vst[:, vi, kk, :], pT[:, pi, 0:N], kt == 0, kt == nkt - 1)],
                     [vn, pn], [PS(ba)])
            if kt == nkt - 1:
                ri, rn = rdrot.next()
                if h == NH - 1:
                    act(rden[0:64, ri, :], ps[64:128, ba, :], AF.Ln, [PS(ba)], [rn])
                    act(rden[0:64, ri, :], rden[0:64, ri, :], AF.Exp, [rn], [rn], scale=-1.0)
                else:
                    dve(lambda e, ri=ri, ba=ba: e.reciprocal(rden[0:64, ri, :], ps[64:128, ba, :]), [PS(ba)], [rn])
                mv, mn = MIX(h // 2, hp, hp + 64)
                dve(lambda e, ri=ri, ba=ba, mv=mv: e.tensor_tensor(out=mv, in0=ps[0:64, ba, :],
                                                                  in1=rden[0:64, ri, :], op=ALU.mult),
                    [PS(ba), rn, mn], [mn])

        SK = 2
        for n in range(len(steps) + SK):
            if n < len(steps):
                qk(steps[n])
            if n - SK >= 0:
                pv(steps[n - SK])
            if n % 3 == 0:
                drain_late(1)
        for b in (6, 7):
            held.discard(b)

    def load_x(z, i):
        dma("pool", xr[:, z, :, :], xT[z, :, :, i * T:i * T + T], [], [("xr", z, c) for c in range(KC)], ("x", z))

    def cast_x(z):
        for c in range(0, KC, 2):
            dve(lambda e, c=c, z=z: e.tensor_copy(xb[:, z, c:c + 2, :], xr[:, z, c:c + 2, :]),
                [("xr", z, c), ("xr", z, c + 1)], [("xb", z, c), ("xb", z, c + 1)])

    need_cast = set()

    def finish_tile(z, i):
        drain_late(stream=z)
        dma("pool", outT[z, :, :, i * T:i * T + T], xr[:, z, :, :], [("xr", z, c) for c in range(KC)], [], ("out", z))
        if i + 1 < NT:
            load_x(z, i + 1)
            need_cast.add(z)

    for z in range(NSEQ):
        load_x(z, 0)
    for z in range(NSEQ):
        cur[0] = z
        seq_setup(z)
        cast_x(z)
    for i in range(NT):
        t0 = i * T
        for z in range(NSEQ):
            cur[0] = z
            drain_late(stream=z)
            if z in need_cast:
                need_cast.discard(z)
                cast_x(z)
            layer0_mixer(first=(i == 0))
            if NSEQ == 2 and z == 0 and i > 0:
                finish_tile(1, i - 1)
                cur[0] = z
            mem_attention(0)
            layer0_glinear()
            wout_ln(["AOUT0", "AOUT1"], 0)
        for z in range(NSEQ):
            cur[0] = z
            drain_late(stream=z)
            ffn(0)
            if debug:
                drain_late(stream=z)
            if debug:
                dma("sp", dbgT[z, :, :, t0:t0 + T], xr[:, z, :, :], [XRN(c) for c in range(KC)], [], ("dbg", z))
        for z in range(NSEQ):
            cur[0] = z
            drain_late(stream=z)
            layer1_proj(z, i)
            pre = fox_prefetch(z, i)
            mem_attention(1)
            fox_attention(z, i, pre)
            wout_ln(["BOUT0", "BOUT1"], 1)
        for z in range(NSEQ):
            cur[0] = z
            drain_late(stream=z)
            if NSEQ == 2 and z == 1:
                def hook(i=i):
                    finish_tile(0, i)
                    cur[0] = 1
                ffn(1, after_lnbegin=hook)
            else:
                ffn(1)
            if NSEQ == 1:
                finish_tile(z, i)
    if NSEQ == 2:
        finish_tile(1, NT - 1)

    drain_late()
    assert ring.consumed == len(plan), (ring.consumed, len(plan))
    sch.emit(nc, stack)
    stack.close()
    return nc, sch


def _unit(Wsub):
    n = Wsub.shape[1]
    a = np.ascontiguousarray(Wsub.reshape(KC, 128, n).transpose(1, 0, 2)).reshape(128, KC * n)
    out = np.zeros((128, USZ), np.float32)
    out[:, :KC * n] = a
    return out


def _host_prepare(inp):
    f = np.float32
    units = np.zeros((NU, 128, USZ), f)
    a_w_in, a_w_out = inp["a_w_in"][0], inp["a_w_out"][0]
    b_w_q, b_w_out = inp["b_w_q"][0], inp["b_w_out"][0]
    kv_w = inp["kv_w"]
    for hfi in range(2):
        units[UIDX["AIN%d" % hfi]] = _unit(a_w_in[:, 512 * hfi:512 * hfi + 512])
        units[UIDX["AOUT%d" % hfi]] = _unit(a_w_out[:, 512 * hfi:512 * hfi + 512])
        units[UIDX["BOUT%d" % hfi]] = _unit(b_w_out[:, 512 * hfi:512 * hfi + 512])
    for l in range(2):
        Wup = inp["ffn_w_up"][l]
        Wd = inp["ffn_w_down"][l]
        for i in range(11):
            blk = np.zeros((D, 2, 256), f)
            c0, c1 = 256 * i, min(256 * i + 256, DFF)
            blk[:, 0, :c1 - c0] = Wup[:, c0:c1]
            blk[:, 1, :c1 - c0] = Wup[:, DFF + c0:DFF + c1]
            units[UIDX["UP%d_%d" % (l, i)]] = _unit(blk.reshape(D, 512))
        Wdp = np.zeros((NPC * 128, D), f)
        Wdp[:DFF] = Wd
        Wdp = Wdp.reshape(NPC, 128, D)
        for hf in range(2):
            for gi in range(3):
                blk = np.zeros((8, 128, 512), f)
                n = min(8, NPC - 8 * gi)
                blk[:n] = Wdp[8 * gi:8 * gi + n, :, 512 * hf:512 * hf + 512]
                units[UIDX["DN%d_%d_%d" % (l, hf, gi)]] = np.ascontiguousarray(blk.transpose(1, 0, 2)).reshape(128, USZ)
        units[UIDX["MKV%d" % l]] = _unit(inp["mem_w_kv"][l])
    for k in range(3):
        units[UIDX["KV%d" % k]] = _unit(kv_w[:, 512 * k:512 * k + 512])
    for k in range(2):
        blk = np.zeros((D, 6, 65), f)
        blk[:, :, :64] = b_w_q[:, 384 * k:384 * k + 384].reshape(D, 6, 64)
        units[UIDX["BQ%d" % k]] = _unit(blk.reshape(D, 390))
    units[UIDX["BQM"]] = _unit(b_w_q[:, 768:1024])

    cols = np.zeros((128, NCOLS), f)
    for l in range(2):
        for nm, key in (("ln1g", "ln1_g"), ("ln1b", "ln1_b"), ("ln2g", "ln2_g"), ("ln2b", "ln2_b")):
            cols[:, COLMAP[(nm, l)]:COLMAP[(nm, l)] + 8] = inp[key][l].reshape(8, 128).T
        cwp = np.zeros((3, 2, NPC * 128), f)
        cwp[:, :, :DFF] = inp["ffn_conv_w"][l].reshape(3, 2, DFF)
        cbp = np.zeros((2, NPC * 128), f)
        cbp[:, :DFF] = inp["ffn_conv_b"][l].reshape(2, DFF)
        for k in range(3):
            cols[:, COLMAP[("cw", l, k)]:COLMAP[("cw", l, k)] + 44] = cwp[k].reshape(44, 128).T
        cols[:, COLMAP[("cb", l)]:COLMAP[("cb", l)] + 44] = cbp.reshape(44, 128).T
    cols[:, COLMAP["pscale"]:COLMAP["pscale"] + 6] = inp["a_pool_scale"][0].reshape(6, 128).T

    fbc = np.ascontiguousarray(np.broadcast_to(inp["f_b"].astype(f)[None, :], (128, NH)))

    consts = np.zeros((128, CW), f)
    ii = np.arange(128)
    consts[:, C_U:C_U + 128] = (ii[:, None] <= ii[None, :]).astype(f)
    consts[:, C_ID:C_ID + 128] = np.eye(128, dtype=f)
    consts[:, C_MASK:C_MASK + 128] = np.where(ii[:, None] > ii[None, :], NEG, 0.0).astype(f)
    for wi, w in enumerate((2, 4, 8, 16)):
        consts[:, C_INVC + 16 * wi:C_INVC + 16 * wi + 16] = (1.0 / np.minimum(np.arange(16) + 1, w))[None, :]
    for h in range(NH):
        consts[h, C_E + 65 * h + 64] = 8.0
    ss, tt = ii[:, None], ii[None, :]
    for wi, w in enumerate((2, 4, 8, 16)):
        main = np.where((ss <= tt) & (ss > tt - w), 1.0 / w, 0.0) - (ss == tt)
        halo = np.where(ss >= tt - w + 129, 1.0 / w, 0.0)
        first = np.where((ss <= tt) & (ss > tt - w), 1.0 / np.minimum(tt + 1, w), 0.0) - (ss == tt)
        for kind, m in enumerate((main, halo, first)):
            consts[:, C_B + (wi * 3 + kind) * 128:C_B + (wi * 3 + kind) * 128 + 128] = m.astype(f)

    pw = inp["a_pool_w"][0]
    wfull = np.zeros((768, 768), f)
    for g in range(4):
        wfull[192 * g:192 * g + 192, 192 * g:192 * g + 192] = pw[g]
    tiles_ = []
    for oc, srcs in {0: [0, 1], 1: [0, 1, 2], 2: [1, 2], 3: [3, 4], 4: [3, 4, 5], 5: [4, 5]}.items():
        for ch in srcs:
            tiles_.append(wfull[128 * ch:128 * ch + 128, 128 * oc:128 * oc + 128])
    wp = np.ascontiguousarray(np.stack(tiles_, axis=1)).reshape(128, 14 * 128)
    wfh = np.ascontiguousarray(kv_w[:, 1536:1548].reshape(KC, 128, NH).transpose(1, 0, 2)).reshape(128, KC * NH)
    return dict(wunits=units, cols=cols, fbc=fbc, consts=consts, wp=wp, wf=wfh)


def _to_fm(a):
    n, t, _ = a.shape
    return np.ascontiguousarray(a.reshape(n, t, KC, 128).transpose(0, 3, 2, 1))


def _from_fm(a):
    n, _, _, t = a.shape
    return np.ascontiguousarray(a.transpose(0, 3, 2, 1)).reshape(n, t, D)


_CACHE = {}


def run(inputs, n_cores, nseq, S, debug=False):
    inp = {k: np.asarray(v, dtype=np.float32) for k, v in inputs.items()}
    shared = _host_prepare(inp)
    key = (nseq, S, debug)
    if key not in _CACHE:
        _CACHE[key] = build_program(nseq, S, debug)[0]
    nc = _CACHE[key]
    in_maps = []
    for c in range(n_cores):
        m = dict(shared)
        m["xT"] = _to_fm(inp["x"][c * nseq:(c + 1) * nseq])
        m["memT"] = _to_fm(inp["mem"][c * nseq:(c + 1) * nseq])
        in_maps.append(m)
    res = run_bass_kernel_spmd(nc, in_maps, core_ids=list(range(n_cores)))
    out = np.concatenate([_from_fm(r["outT"]) for r in res.results], axis=0)
    if debug:
        dbg = np.concatenate([_from_fm(r["dbgT"]) for r in res.results], axis=0)
        return out, dbg
    return out


def kernel(**inputs):
    B, S, _ = inputs["x"].shape
    n_cores = 8
    return run(inputs, n_cores, B // n_cores, S).astype(np.float32)
```
